# Optimizing a Trainium2 kernel written in Bass

```python
import math
import jax
import jax.numpy as jnp
from jax import lax
import numpy as np

D_MODEL = 2048
BATCH = 4
SEQ = 4096
DEPTH = 1

GDN_HEADS = 8
GDN_HEAD_DIM = 128
GDN_WIDTH = GDN_HEADS * GDN_HEAD_DIM
GDN_CONV = 4
GDN_CHUNK = 64

MOBA_HEADS = 8
MOBA_HEAD_DIM = 128
MOBA_WIDTH = MOBA_HEADS * MOBA_HEAD_DIM
MOBA_BLOCK = 256
MOBA_TOPK = 3
MOBA_QBLOCK = 64
ROPE_THETA = 500000.0
ROPE_DIM = MOBA_HEAD_DIM // 4

D_FF = 5632
FFN_CONV = 3

NORM_EPS = 1e-6
NEG_INF = -1e30

IN_SPLITS = (3 * GDN_WIDTH, GDN_HEADS, GDN_HEADS, GDN_WIDTH, 3 * MOBA_WIDTH, D_MODEL, D_MODEL)
IN_DIM = sum(IN_SPLITS)

kernel_name = "hybrid_gdn_moba_convffn"


def rms_norm(x, w):
    xf = x.astype(jnp.float32)
    y = xf * lax.rsqrt(jnp.mean(xf * xf, axis=-1, keepdims=True) + NORM_EPS)
    return (y * w.astype(jnp.float32)).astype(x.dtype)


def l2norm(x):
    return x * lax.rsqrt(jnp.sum(x * x, axis=-1, keepdims=True) + 1e-6)


def causal_dwconv(x, w, bias=None):
    width, ch = w.shape
    y = lax.conv_general_dilated(
        x, w[:, None, :].astype(x.dtype), window_strides=(1,), padding=[(width - 1, 0)],
        dimension_numbers=('NWC', 'WIO', 'NWC'), feature_group_count=ch)
    if bias is not None:
        y = y + bias.astype(x.dtype)
    return y


def partial_rope(x, positions):
    half = ROPE_DIM // 2
    inv_freq = ROPE_THETA ** (-jnp.arange(half, dtype=jnp.float32) / half)
    ang = positions.astype(jnp.float32)[..., None] * inv_freq
    cos = jnp.cos(ang)[:, :, None, :]
    sin = jnp.sin(ang)[:, :, None, :]
    xr = x[..., :ROPE_DIM].astype(jnp.float32)
    x1, x2 = xr[..., :half], xr[..., half:]
    rot = jnp.concatenate([x1 * cos - x2 * sin, x2 * cos + x1 * sin], axis=-1).astype(x.dtype)
    return jnp.concatenate([rot, x[..., ROPE_DIM:]], axis=-1)


def chunk_gated_delta_rule(q, k, v, g, beta):
    B, S, H, Dk = q.shape
    Dv = v.shape[-1]
    C = GDN_CHUNK
    N = S // C

    def chunks(t):
        return jnp.moveaxis(t.reshape((B, N, C, H) + t.shape[3:]), 3, 1)

    q = chunks(q) * (Dk ** -0.5)
    k = chunks(k)
    v = chunks(v)
    g = chunks(g)
    beta = chunks(beta)
    gc = jnp.cumsum(g, axis=-1)
    r = jnp.arange(C)
    causal = r[:, None] >= r[None, :]
    strict = r[:, None] > r[None, :]
    gamma = jnp.exp(jnp.where(causal, gc[..., :, None] - gc[..., None, :], -jnp.inf))
    kb = k * beta[..., None]
    a_mat = jnp.where(strict, jnp.einsum('bhncd,bhnsd->bhncs', kb, k) * gamma, 0.0)
    eye = jnp.eye(C, dtype=q.dtype)
    t_mat = lax.linalg.triangular_solve(eye + a_mat, jnp.broadcast_to(eye, a_mat.shape),
                                        left_side=True, lower=True, unit_diagonal=True)
    u = jnp.einsum('bhncs,bhnsv->bhncv', t_mat, v * beta[..., None])
    w = jnp.einsum('bhncs,bhnsk->bhnck', t_mat, kb * jnp.exp(gc)[..., None])
    qk = jnp.einsum('bhncd,bhnsd->bhncs', q, k) * gamma
    qg = q * jnp.exp(gc)[..., None]
    kd = k * jnp.exp(gc[..., -1:] - gc)[..., None]
    glast = jnp.exp(gc[..., -1])

    def step(state, xs):
        u_n, w_n, qg_n, qk_n, kd_n, gl_n = xs
        v_new = u_n - jnp.einsum('bhck,bhkv->bhcv', w_n, state)
        o = jnp.einsum('bhck,bhkv->bhcv', qg_n, state) + jnp.einsum('bhcs,bhsv->bhcv', qk_n, v_new)
        state = state * gl_n[..., None, None] + jnp.einsum('bhck,bhcv->bhkv', kd_n, v_new)
        return state, o

    xs = tuple(jnp.moveaxis(t, 2, 0) for t in (u, w, qg, qk, kd, glast))
    s0 = jnp.zeros((B, H, Dk, Dv), q.dtype)
    _, o = lax.scan(step, s0, xs)
    return o.transpose(1, 0, 3, 2, 4).reshape(B, S, H, Dv)


def gated_deltanet(qkv, beta_logit, a_logit, z, conv_w, a_log, dt_bias, norm_w):
    B, S, _ = qkv.shape
    dtype = qkv.dtype
    f32 = jnp.float32
    qkv = jax.nn.silu(causal_dwconv(qkv, conv_w))
    q, k, v = jnp.split(qkv.astype(f32), 3, axis=-1)
    shp = (B, S, GDN_HEADS, GDN_HEAD_DIM)
    q = l2norm(q.reshape(shp))
    k = l2norm(k.reshape(shp))
    v = v.reshape(shp)
    beta = jax.nn.sigmoid(beta_logit.astype(f32))
    g = -jnp.exp(a_log.astype(f32)) * jax.nn.softplus(a_logit.astype(f32) + dt_bias.astype(f32))
    o = chunk_gated_delta_rule(q, k, v, g, beta)
    o = rms_norm(o, norm_w) * jax.nn.silu(z.astype(f32).reshape(shp))
    return o.reshape(B, S, GDN_WIDTH).astype(dtype)


def moba_attention(q, k, v):
    B, S, H, D = q.shape
    BLK = MOBA_BLOCK
    NB = -(-S // BLK)
    Sp = NB * BLK
    K = min(MOBA_TOPK, NB)
    QB = MOBA_QBLOCK
    NQ = Sp // QB
    pad = ((0, 0), (0, Sp - S), (0, 0), (0, 0))
    qh = jnp.pad(q, pad).transpose(0, 2, 1, 3) * (D ** -0.5)
    kb = jnp.pad(k, pad).transpose(0, 2, 1, 3).reshape(B, H, NB, BLK, D)
    vb = jnp.pad(v, pad).transpose(0, 2, 1, 3).reshape(B, H, NB, BLK, D)
    k_mean = jnp.mean(kb.astype(jnp.float32), axis=3)
    gate = jnp.einsum('bhsd,bhnd->bhsn', qh.astype(jnp.float32), k_mean)
    q_blk = jnp.arange(Sp) // BLK
    fully_past = jnp.arange(NB)[None, :] < q_blk[:, None]
    gate = jnp.where(fully_past, gate, -jnp.inf)
    _, sel = lax.top_k(gate, K)
    qs = qh.reshape(B, H, NQ, QB, D).transpose(0, 2, 1, 3, 4).reshape(B * NQ, H, QB, D)
    ss = sel.reshape(B, H, NQ, QB, K).transpose(0, 2, 1, 3, 4).reshape(B * NQ, H, QB, K)
    b_idx = jnp.repeat(jnp.arange(B, dtype=jnp.int32), NQ)
    n_idx = jnp.tile(jnp.arange(NQ, dtype=jnp.int32), B)
    head = jnp.arange(H)[:, None, None]

    def query_block(args):
        qn, sn, b, n = args
        kb_b = kb[b]
        vb_b = vb[b]
        k_sel = kb_b[head, sn]
        v_sel = vb_b[head, sn]
        t = n * QB + jnp.arange(QB)
        own = (n * QB) // BLK
        k_own = lax.dynamic_index_in_dim(kb_b, own, axis=1, keepdims=False)
        v_own = lax.dynamic_index_in_dim(vb_b, own, axis=1, keepdims=False)
        s_past = jnp.einsum('hqd,hqkld->hqkl', qn, k_sel, preferred_element_type=jnp.float32)
        slot_ok = jnp.arange(K)[None, :] < (t // BLK)[:, None]
        s_past = jnp.where(slot_ok[None, :, :, None], s_past, NEG_INF).reshape(H, QB, K * BLK)
        s_own = jnp.einsum('hqd,hld->hql', qn, k_own, preferred_element_type=jnp.float32)
        own_ok = (own * BLK + jnp.arange(BLK))[None, :] <= t[:, None]
        s_own = jnp.where(own_ok[None], s_own, NEG_INF)
        p = jax.nn.softmax(jnp.concatenate([s_past, s_own], axis=-1), axis=-1).astype(v.dtype)
        p_past = p[..., :K * BLK].reshape(H, QB, K, BLK)
        p_own = p[..., K * BLK:]
        return (jnp.einsum('hqkl,hqkld->hqd', p_past, v_sel)
                + jnp.einsum('hql,hld->hqd', p_own, v_own))

    out = lax.map(query_block, (qs, ss, b_idx, n_idx))
    out = out.reshape(B, NQ, H, QB, D).transpose(0, 1, 3, 2, 4).reshape(B, Sp, H, D)
    return out[:, :S]


def split_columns(proj):
    offs = [int(o) for o in np.cumsum(IN_SPLITS)[:-1]]
    return jnp.split(proj, offs, axis=-1)


def setup_inputs(seed: int = 0) -> dict:
    key = jax.random.key(seed)
    ks = jax.random.split(key, 20)
    f32 = jnp.float32
    L = DEPTH

    def nrm(k, shape, scale):
        return jax.random.normal(k, shape, f32) * scale

    x = nrm(ks[0], (BATCH, SEQ, D_MODEL), 1.0)
    positions = jnp.broadcast_to(jnp.arange(SEQ, dtype=jnp.int32), (BATCH, SEQ))
    ln1 = 1.0 + nrm(ks[1], (L, D_MODEL), 0.02)
    w_in = nrm(ks[2], (L, D_MODEL, IN_DIM), D_MODEL ** -0.5)
    gdn_conv = nrm(ks[3], (L, GDN_CONV, 3 * GDN_WIDTH), GDN_CONV ** -0.5)
    gdn_a_log = jnp.log(jax.random.uniform(ks[4], (L, GDN_HEADS), f32, 1.0, 16.0))
    dt = jnp.exp(jax.random.uniform(ks[5], (L, GDN_HEADS), f32, math.log(1e-3), math.log(1e-1)))
    gdn_dt_bias = dt + jnp.log(-jnp.expm1(-dt))
    gdn_norm = 1.0 + nrm(ks[6], (L, GDN_HEAD_DIM), 0.02)
    w_branch_a = nrm(ks[7], (L, GDN_WIDTH, D_MODEL), GDN_WIDTH ** -0.5)
    w_branch_b = nrm(ks[8], (L, MOBA_WIDTH, D_MODEL), MOBA_WIDTH ** -0.5)
    w_out = nrm(ks[9], (L, D_MODEL, D_MODEL), D_MODEL ** -0.5)
    ln2 = 1.0 + nrm(ks[10], (L, D_MODEL), 0.02)
    w_up = nrm(ks[11], (L, D_MODEL, 2 * D_FF), D_MODEL ** -0.5)
    ffn_conv = nrm(ks[12], (L, FFN_CONV, 2 * D_FF), FFN_CONV ** -0.5)
    ffn_conv_bias = nrm(ks[13], (L, 2 * D_FF), 0.01)
    w_down = nrm(ks[14], (L, D_FF, D_MODEL), D_FF ** -0.5)
    final_norm = 1.0 + nrm(ks[15], (D_MODEL,), 0.02)
    return {"x": x, "positions": positions, "ln1": ln1, "w_in": w_in, "gdn_conv": gdn_conv,
            "gdn_a_log": gdn_a_log, "gdn_dt_bias": gdn_dt_bias, "gdn_norm": gdn_norm,
            "w_branch_a": w_branch_a, "w_branch_b": w_branch_b, "w_out": w_out, "ln2": ln2,
            "w_up": w_up, "ffn_conv": ffn_conv, "ffn_conv_bias": ffn_conv_bias,
            "w_down": w_down, "final_norm": final_norm}


def reference(x, positions, ln1, w_in, gdn_conv, gdn_a_log, gdn_dt_bias, gdn_norm,
              w_branch_a, w_branch_b, w_out, ln2, w_up, ffn_conv, ffn_conv_bias, w_down,
              final_norm):
    B, S, _ = x.shape
    for l in range(DEPTH):
        h = rms_norm(x, ln1[l])
        proj = h @ w_in[l]
        qkv_a, beta_a, decay_a, z_a, qkv_b, gate_a, gate_b = split_columns(proj)
        y_a = gated_deltanet(qkv_a, beta_a, decay_a, z_a, gdn_conv[l], gdn_a_log[l],
                             gdn_dt_bias[l], gdn_norm[l]) @ w_branch_a[l]
        q_b, k_b, v_b = jnp.split(qkv_b, 3, axis=-1)
        shp = (B, S, MOBA_HEADS, MOBA_HEAD_DIM)
        q_b = partial_rope(q_b.reshape(shp), positions)
        k_b = partial_rope(k_b.reshape(shp), positions)
        y_b = moba_attention(q_b, k_b, v_b.reshape(shp)).reshape(B, S, MOBA_WIDTH) @ w_branch_b[l]
        merged = jax.nn.sigmoid(gate_a) * y_a + jax.nn.sigmoid(gate_b) * y_b
        x = x + merged @ w_out[l]
        h = rms_norm(x, ln2[l])
        u = causal_dwconv(h @ w_up[l], ffn_conv[l], ffn_conv_bias[l])
        u_gate, u_val = jnp.split(u, 2, axis=-1)
        x = x + (jax.nn.silu(u_gate) * u_val) @ w_down[l]
    return rms_norm(x, final_norm)
```

```python
import numpy as np
from contextlib import ExitStack
import concourse.bass as bass
import concourse.mybir as mybir
from concourse.bass_utils import run_bass_kernel_spmd

F32 = mybir.dt.float32
BF16 = mybir.dt.bfloat16
I32 = mybir.dt.int32
AF = mybir.ActivationFunctionType
ALU = mybir.AluOpType
AX = mybir.AxisListType

import os as _os
SAME_ENGINE_SYNC = _os.environ.get("SAME_ENGINE_SYNC", "1") == "1"

D = 2048
L = 4096
Q0 = 1920
NQ = 2176
IN_DIM = 11280
DFF = 5632
BIG = 1.0e30
EPS = 1e-6
C_QKVA, C_BETA, C_DEC, C_Z, C_QKVB, C_GA, C_GB = 0, 3072, 3080, 3088, 4112, 7184, 9232

QTILES = [(0, 128), (128, 512), (640, 512), (1152, 512), (1664, 512)]
CTXTILES = [(0, 512), (512, 512), (1024, 512), (1536, 384)]


class Buf:
    __slots__ = ("name", "w", "r")

    def __init__(self, name=""):
        self.name = name
        self.w = None
        self.r = []


class FW:
    def __init__(self, nc, stack):
        self.nc = nc
        self.stack = stack
        self.eng = {"pe": nc.tensor, "act": nc.scalar, "dve": nc.vector, "pool": nc.gpsimd, "sp": nc.sync}
        self.sem, self.cnt, self.waited = {}, {}, {}
        for k in self.eng:
            self.sem[k] = stack.enter_context(nc.semaphore("s_" + k))
            self.cnt[k] = 0
            self.waited[k] = {}
        self.dsem, self.dcnt = {}, {}
        self.nwait = 0
        self.ninst = 0
        self.uid = 0
        self.deferred = []

    def sb(self, name, shape, dt=F32, stack=None):
        self.uid += 1
        return (stack or self.stack).enter_context(self.nc.sbuf_tensor(f"{name}_{self.uid}", list(shape), dt))

    def ps(self, name, shape, dt=F32, stack=None):
        self.uid += 1
        return (stack or self.stack).enter_context(self.nc.psum_tensor(f"{name}_{self.uid}", list(shape), dt))

    def dma_sem(self, key):
        if key not in self.dsem:
            self.dsem[key] = self.stack.enter_context(self.nc.semaphore("d_" + key))
            self.dcnt[key] = 0
        return key

    def _semh(self, key):
        return self.sem[key] if key in self.sem else self.dsem[key]

    def _wait(self, ek, k, v):
        if self.waited[ek].get(k, 0) >= v:
            return
        self.eng[ek].wait_ge(self._semh(k), v)
        self.nwait += 1
        self.waited[ek][k] = v

    def _deps(self, ek, reads, writes):
        deps = {}
        for b in reads:
            if b.w is not None:
                k, v = b.w
                deps[k] = max(deps.get(k, 0), v)
        for b in writes:
            if b.w is not None:
                k, v = b.w
                deps[k] = max(deps.get(k, 0), v)
            for (k, v) in b.r:
                deps[k] = max(deps.get(k, 0), v)
        for k, v in deps.items():
            if k in self.dcnt:
                v = self.dcnt[k]
            if k == ek:
                if (not SAME_ENGINE_SYNC) or ek in ("pe", "sp") or v > self.cnt[ek]:
                    continue
            self._wait(ek, k, v)

    def flush(self):
        d, self.deferred = self.deferred, []
        for (qk, out, in_, reads, writes, key, kw) in d:
            self.dma(qk, out, in_, reads=reads, writes=writes, key=key, **kw)

    def _guard(self, writes):
        if self.deferred:
            for d in self.deferred:
                if any(b is w for b in d[3] for w in writes):
                    self.flush()
                    return

    def op(self, ek, fn, reads=(), writes=(), inc=True):
        self._guard(writes)
        self._deps(ek, reads, writes)
        ins = fn(self.eng[ek])
        self.ninst += 1
        if inc:
            ins.then_inc(self.sem[ek], 1)
            self.cnt[ek] += 1
            tag = (ek, self.cnt[ek])
        else:
            tag = (ek, self.cnt[ek] + 1)
        for b in writes:
            b.w = tag
            b.r = []
        for b in reads:
            b.r.append(tag)
            if len(b.r) > 64:
                b.r = self._compact(b.r)
        return ins

    @staticmethod
    def _compact(r):
        m = {}
        for k, v in r:
            m[k] = max(m.get(k, 0), v)
        return list(m.items())

    def dma(self, qk, out, in_, reads=(), writes=(), key=None, defer=False, **kw):
        if defer:
            self.deferred.append((qk, out, in_, reads, writes, key, kw))
            return None
        self._guard(writes)
        self.dma_sem(key)
        self._deps(qk, reads, writes)
        ins = self.eng[qk].dma_start(out=out, in_=in_, **kw)
        self.ninst += 1
        ins.then_inc(self.dsem[key], 16)
        self.dcnt[key] += 16
        tag = (key, self.dcnt[key])
        for b in writes:
            b.w = tag
            b.r = []
        for b in reads:
            b.r.append(tag)
            if len(b.r) > 64:
                b.r = self._compact(b.r)
        return ins

    def barrier(self):
        self.flush()
        for ek in self.eng:
            for k in self.eng:
                if k != ek and self.cnt[k] > 0:
                    self._wait(ek, k, self.cnt[k])
            for k, v in self.dcnt.items():
                if v > 0:
                    self._wait(ek, k, v)


class Ring:
    def __init__(self, fw, name, n, shape, dt, stack):
        self.t = [fw.sb(f"{name}{i}", shape, dt, stack) for i in range(n)]
        self.b = [Buf(f"{name}{i}") for i in range(n)]
        self.i = 0
        self.n = n

    def next(self):
        i = self.i
        self.i = (i + 1) % self.n
        return self.t[i], self.b[i], i


def build_consts():
    c = {}
    p = np.arange(128)
    eye = np.eye(128, dtype=np.float32)
    c["ident"] = eye
    c["ones"] = np.ones((128, 128), np.float32)
    c["tri"] = (p[:, None] <= p[None, :]).astype(np.float32)
    c["mbS"] = np.where(p[None, :] < p[:, None], 0.0, BIG).astype(np.float32)
    c["mbU"] = np.where(p[None, :] >= p[:, None], 0.0, -BIG).astype(np.float32)
    q = np.arange(512)
    cb = [np.where(p[:, None] + 128 * o <= q[None, :], 0.0, -BIG).astype(np.float32) for o in range(4)]
    c["cb"] = np.concatenate(cb, axis=1)
    pm = np.zeros((32, 32), np.float32)
    for i in range(16):
        pm[i + 16, i] = -1.0
        pm[i, i + 16] = 1.0
    c["pm"] = np.pad(pm, ((0, 96), (0, 0)))
    half = 16
    inv = (500000.0 ** (-np.arange(half, dtype=np.float32) / half)).astype(np.float32)
    invp = np.zeros((128, 1), np.float32)
    invp[:32, 0] = np.concatenate([inv, inv])
    c["invf"] = invp
    E = np.zeros((16, 16, 128), np.float32)
    for n in range(16):
        E[n, n, :] = 1.0
    c["E"] = np.pad(E.reshape(16, 2048), ((0, 112), (0, 0)))
    names = ["ident", "ones", "tri", "mbS", "mbU", "cb", "pm", "invf", "E"]
    offs, cols = {}, 0
    for n in names:
        offs[n] = (cols, c[n].shape[1])
        cols += c[n].shape[1]
    arr = np.concatenate([c[n] for n in names], axis=1).astype(np.float32)
    return arr, offs


CONST_ARR, CONST_OFFS = build_consts()


class Ctx:
    pass


def build_program(debug=None):
    debug = debug or set()
    stop = [d[5:] for d in debug if d.startswith("stop:")]
    stop = stop[0] if stop else None
    nc = bass.Bass("TRN2", target_bir_lowering=False)
    K = Ctx()
    K.nc = nc

    def din(name, shape, dt=F32):
        return nc.dram_tensor(name, list(shape), dt, kind="ExternalInput").ap()

    def dscr(name, shape, dt=F32):
        kind = "ExternalOutput" if name in debug else "Internal"
        return nc.dram_tensor(name, list(shape), dt, kind=kind).ap()

    K.x = din("x", [L, D])
    K.pos = din("pos", [1, L], I32)
    K.cvalid = din("cvalid", [1, 16])
    K.consts = din("consts", list(CONST_ARR.shape))
    K.ln1 = din("ln1", [1, D]); K.ln2 = din("ln2", [1, D]); K.fnorm = din("final_norm", [1, D])
    K.w_in = din("w_in", [D, IN_DIM])
    K.gdn_conv = din("gdn_conv", [4, 3072])
    K.a_log = din("gdn_a_log", [1, 8]); K.dt_bias = din("gdn_dt_bias", [1, 8]); K.gdn_norm = din("gdn_norm", [1, 128])
    K.w_a = din("w_branch_a", [1024, D]); K.w_b = din("w_branch_b", [1024, D]); K.w_out = din("w_out", [D, D])
    K.w_up = din("w_up", [D, 2 * DFF]); K.ffn_conv = din("ffn_conv", [3, 2 * DFF]); K.ffn_bias = din("ffn_conv_bias", [1, 2 * DFF])
    K.w_down = din("w_down", [DFF, D])
    K.out = nc.dram_tensor("out", [2048, D], F32, kind="ExternalOutput").ap()
    K.qkvA = dscr("qkvA", [3072, L])
    K.zTM = dscr("zTM", [NQ, 1024])
    K.qkB = dscr("qkB", [2048, L])
    K.vB = dscr("vB", [L, 1024], BF16)
    K.gA = dscr("gA", [D, NQ]); K.gB = dscr("gB", [D, NQ])
    K.qnT = dscr("qnT", [1024, NQ], BF16); K.knT = dscr("knT", [1024, L], BF16); K.vsT = dscr("vsT", [1024, L], BF16)
    K.oaT = dscr("oaT", [1024, NQ], BF16); K.obT = dscr("obT", [1024, NQ], BF16)
    K.mT = dscr("mT", [17, 128, 16, 128], BF16)
    K.x2 = dscr("x2", [NQ, D])
    K.uT = dscr("uT", [17, 128, 44, 128], BF16)
    K.x3 = dscr("x3", [2048, D])
    K.dbg = dscr("dbg", [128, 4096])
    K.B = {n: Buf(n) for n in ["qkvA", "zTM", "qkB", "vB", "gA", "gB", "qnT", "knT", "vsT", "oaT", "obT", "mT", "x2", "uT", "x3", "out", "dbg"]}

    import os
    with ExitStack() as gst:
        fw = FW(nc, gst)
        K.fw = fw
        cw = CONST_ARR.shape[1]
        K.cst = fw.sb("cst", [128, cw])
        K.cstb = Buf("cst")
        fw.dma("sp", K.cst[:], K.consts[:, :], writes=[K.cstb], key="cst")

        def cs(name, rows=128):
            o, n = CONST_OFFS[name]
            return K.cst[0:rows, o:o + n]
        K.cs = cs
        K.identb = fw.sb("identb", [128, 128], BF16); K.onesb = fw.sb("onesb", [128, 128], BF16)
        K.cbb = fw.sb("cbb", [128, 2048], BF16); K.Eb = fw.sb("Eb", [16, 2048], BF16)
        K.cbuf = Buf("constsb")
        fw.op("dve", lambda e: e.tensor_copy(K.identb[:], cs("ident")), reads=[K.cstb], writes=[K.cbuf])
        fw.op("dve", lambda e: e.tensor_copy(K.onesb[:], cs("ones")), reads=[K.cstb], writes=[K.cbuf])
        fw.op("dve", lambda e: e.tensor_copy(K.cbb[:], cs("cb")), reads=[K.cstb], writes=[K.cbuf])
        fw.op("dve", lambda e: e.tensor_copy(K.Eb[:], cs("E", 16)), reads=[K.cstb], writes=[K.cbuf])
        K.bd = fw.sb("bd", [128, 32, 16]); K.bdb = Buf("bd")
        K.epsc = fw.sb("epsc", [128, 1]); K.epsb = Buf("eps")
        fw.op("pool", lambda e: e.memset(K.epsc[:], EPS), writes=[K.epsb])
        if os.environ.get("SKIP_PHASES"):
            fw.op("pool", lambda e: e.memset(K.bd[:], 0.0), writes=[K.bdb])

        phases = [("inproj", phase_inproj), ("gdnprep", phase_gdn_prep), ("gdn", phase_gdn), ("moba", phase_moba),
                  ("merge", phase_merge), ("outproj", phase_outproj), ("ffnup", phase_ffn_up), ("ffndown", phase_ffn_down),
                  ("final", phase_final)]
        import os
        skip = os.environ.get("SKIP_PHASES", "").split(",")
        for name, fn in phases:
            if name in skip:
                continue
            with ExitStack() as pst:
                fn(K, fw, pst)
                fw.barrier()
            if stop == name:
                break
        print(f"[build] instructions={fw.ninst} waits={fw.nwait} cnt={fw.cnt} dsems={len(fw.dsem)}")
    return nc


def evac_engines():
    i = 0
    while True:
        yield "act" if i % 2 == 0 else "dve"
        i += 1


def norm_to_hT(K, fw, st, x_dram, rows, ln_dram, hT, hTb, ps_pair, key):
    lnT = fw.sb(key + "lnT", [128, 16], F32, st); lnb = Buf("lnT")
    fw.dma("sp", lnT[:], ln_dram[0, :].rearrange("(c p) -> p c", p=128), writes=[lnb], key=key + "ln",
           allow_slow_non_contiguous=True)
    xr = Ring(fw, key + "x", 2, [128, D], F32, st)
    xn = Ring(fw, key + "xn", 2, [128, D], BF16, st)
    junk = fw.sb(key + "junk", [128, D], BF16, st); junkb = Buf("junk")
    ss = Ring(fw, key + "ss", 2, [128, 4], F32, st)
    for i, r0 in enumerate(rows):
        xt, xb, xi = xr.next()
        fw.dma("sp", xt[:], x_dram[r0:r0 + 128, :], writes=[xb], key=f"{key}x{xi}")
        sst, ssb, _ = ss.next()
        fw.op("act", lambda e: e.activation(out=junk[:], in_=xt[:], func=AF.Square, accum_out=sst[:, 0:1]),
              reads=[xb], writes=[junkb, ssb])
        fw.op("act", lambda e: e.activation(out=sst[:, 1:2], in_=sst[:, 0:1], func=AF.Ln, scale=1.0 / D, bias=K.epsc[:, 0:1]),
              reads=[ssb, K.epsb], writes=[ssb])
        fw.op("act", lambda e: e.activation(out=sst[:, 2:3], in_=sst[:, 1:2], func=AF.Exp, scale=-0.5), reads=[ssb], writes=[ssb])
        xnt, xnb, _ = xn.next()
        fw.op("dve", lambda e: e.tensor_scalar(xnt[:], xt[:], sst[:, 2:3], None, ALU.mult), reads=[xb, ssb], writes=[xnb])
        pt, ptb = ps_pair[i % 2]
        for kc in range(16):
            fw.op("pe", lambda e: e.transpose(pt[:, kc * 128:(kc + 1) * 128], xnt[:, kc * 128:(kc + 1) * 128], K.identb[:]),
                  reads=[xnb, K.cbuf], writes=[ptb], inc=(kc == 15))
        fw.op("dve", lambda e: e.tensor_tensor(hT[:, :, i * 128:(i + 1) * 128], pt[:].rearrange("p (k t) -> p k t", k=16),
                                              lnT[:, :].unsqueeze(2).to_broadcast([128, 16, 128]), ALU.mult),
              reads=[ptb, lnb], writes=[hTb[i]])


def load_wslab(K, fw, wring, W_dram, c0, ncols, nk, key):
    wt, wb, wi = wring.next()
    src = W_dram[:, c0:c0 + ncols].rearrange("(kc p) c -> p kc c", p=128)
    for k0 in range(0, nk, 4):
        k1 = min(nk, k0 + 4)
        fw.dma("pool", wt[:, k0:k1, 0:ncols], src[:, k0:k1, :], writes=[wb] if k0 == 0 else [], key=f"{key}{wi}")
    wb.w = (f"{key}{wi}", fw.dcnt[f"{key}{wi}"])
    return wt, wb


def gemm_fm(K, fw, wt, wb, ncols, nk, act, act_bufs, tiles, psring, evac):
    nj = (ncols + 127) // 128
    for j in range(nj):
        m = min(128, ncols - j * 128)
        for (t0, tn) in tiles:
            pt, pb = psring.next()
            for kc in range(nk):
                fw.op("pe", lambda e: e.matmul(pt[0:m, 0:tn], wt[:, kc, j * 128:j * 128 + m], act[:, kc, t0:t0 + tn],
                                               start=(kc == 0), stop=(kc == nk - 1)),
                      reads=[wb] + act_bufs(t0, tn), writes=[pb], inc=(kc == nk - 1))
            evac(pt, pb, j, m, t0, tn)


def gemm_tm(K, fw, wt, wb, ncols, nk, act, act_bufs, subs, psring, evac):
    for s in subs:
        pt, pb = psring.next()
        for kc in range(nk):
            fw.op("pe", lambda e: e.matmul(pt[:, 0:ncols], act[:, kc, s * 128:(s + 1) * 128], wt[:, kc, 0:ncols],
                                           start=(kc == 0), stop=(kc == nk - 1)),
                  reads=[wb] + act_bufs(s * 128, 128), writes=[pb], inc=(kc == nk - 1))
        evac(pt, pb, s)


class PsRing:
    def __init__(self, fw, st, n, shape=(128, 512), dt=F32, name="ps"):
        self.t = [fw.ps(f"{name}{i}", shape, dt, st) for i in range(n)]
        self.b = [Buf(f"{name}{i}") for i in range(n)]
        self.i = 0

    def next(self):
        i = self.i
        self.i = (i + 1) % len(self.t)
        return self.t[i], self.b[i]


def phase_inproj(K, fw, st):
    hT = fw.sb("hT", [128, 16, NQ], BF16, st)
    hTb = [Buf(f"hT{i}") for i in range(17)]
    wring = Ring(fw, "wsl", 2, [128, 16, 512], BF16, st)
    stg = Ring(fw, "stg", 4, [128, 512], F32, st)
    stgb = Ring(fw, "stgb", 2, [128, 512], BF16, st)
    ev = evac_engines()

    def act_bufs(t0, tn):
        return hTb[t0 // 128:(t0 + tn + 127) // 128]

    def store_fm(dst, dbuf, row0, tok0, func=None):
        def evac(pt, pb, j, m, t0, tn):
            s, sb_, si = stg.next()
            if func is None:
                ek = next(ev)
                if ek == "act":
                    fw.op("act", lambda e: e.activation(out=s[0:m, 0:tn], in_=pt[0:m, 0:tn], func=AF.Copy), reads=[pb], writes=[sb_])
                else:
                    fw.op("dve", lambda e: e.tensor_copy(s[0:m, 0:tn], pt[0:m, 0:tn]), reads=[pb], writes=[sb_])
            else:
                fw.op("act", lambda e: e.activation(out=s[0:m, 0:tn], in_=pt[0:m, 0:tn], func=func), reads=[pb], writes=[sb_])
            fw.dma("sp", dst[row0 + j * 128:row0 + j * 128 + m, tok0 + t0:tok0 + t0 + tn], s[0:m, 0:tn], reads=[sb_], writes=[],
                   key=f"st{si}")
        return evac

    for pas in range(2):
        with ExitStack() as pst:
            if pas == 0:
                rows = [i * 128 for i in range(15)]; tiles = CTXTILES; tok0 = 0; nsub = 15
            else:
                rows = [Q0 + i * 128 for i in range(17)]; tiles = QTILES; tok0 = Q0; nsub = 17
            pp = [(fw.ps(f"ptr{i}", [128, 2048], BF16, pst), Buf(f"ptr{i}")) for i in range(2)]
            norm_to_hT(K, fw, pst, K.x, rows, K.ln1, hT, hTb, pp, f"n1{pas}")
            fw.barrier()
        import os
        BIS = int(os.environ.get("BISECT", "99"))
        if BIS == 1:
            return
        with ExitStack() as pst:
            psr = PsRing(fw, pst, 8)
            jobs = []
            if pas == 1:
                jobs.append(("fm", C_QKVA, 1024, K.qkvA, 0, None))
            jobs.append(("fm", C_QKVA + 1024, 2048, K.qkvA, 1024, None))
            jobs.append(("bd", C_BETA, 16, None, 0, None))
            if pas == 1:
                jobs.append(("ztm", C_Z, 1024, K.zTM, 0, None))
                jobs.append(("fm", C_QKVB, 1024, K.qkB, 0, None))
            jobs.append(("fm", C_QKVB + 1024, 1024, K.qkB, 1024, None))
            jobs.append(("vtm", C_QKVB + 2048, 1024, K.vB, 0, None))
            if pas == 1:
                jobs.append(("fm", C_GA, 2048, K.gA, 0, AF.Sigmoid))
                jobs.append(("fm", C_GB, 2048, K.gB, 0, AF.Sigmoid))
            if BIS < 10:
                jobs = jobs[:BIS - 1]
            if os.environ.get("INPROJ_JOBS"):
                jobs = [j for j in jobs if j[0] in os.environ["INPROJ_JOBS"].split(",")]
            for (mode, wc0, ncols, dst, drow0, func) in jobs:
                for c0 in range(0, ncols, 512):
                    ncs = min(512, ncols - c0)
                    wt, wb = load_wslab(K, fw, wring, K.w_in, wc0 + c0, ncs, 16, "wsl")
                    if mode == "fm":
                        dtok0 = tok0 if dst in (K.qkvA, K.qkB) else 0
                        gemm_fm(K, fw, wt, wb, ncs, 16, hT, act_bufs, tiles, psr,
                                store_fm(dst, K.B, drow0 + c0, dtok0, func))
                    elif mode == "bd":
                        def evac_bd(pt, pb, s):
                            fw.op("dve", lambda e: e.tensor_copy(K.bd[:, tok0 // 128 + s, :], pt[:, 0:16]), reads=[pb], writes=[K.bdb])
                        gemm_tm(K, fw, wt, wb, 16, 16, hT, act_bufs, range(nsub), psr, evac_bd)
                    elif mode == "ztm":
                        def evac_z(pt, pb, s, c0=c0):
                            sg, sb_, si = stg.next()
                            fw.op("act", lambda e: e.activation(out=sg[:, :], in_=pt[:, :], func=AF.Copy), reads=[pb], writes=[sb_])
                            fw.dma("sp", K.zTM[s * 128:(s + 1) * 128, c0:c0 + 512], sg[:, :], reads=[sb_], key=f"st{si}")
                        gemm_tm(K, fw, wt, wb, 512, 16, hT, act_bufs, range(nsub), psr, evac_z)
                    elif mode == "vtm":
                        def evac_v(pt, pb, s, c0=c0):
                            sg, sb_, si = stgb.next()
                            fw.op("dve", lambda e: e.tensor_copy(sg[:, :], pt[:, :]), reads=[pb], writes=[sb_])
                            fw.dma("sp", K.vB[tok0 + s * 128:tok0 + (s + 1) * 128, c0:c0 + 512], sg[:, :], reads=[sb_], key=f"stb{si}")
                        gemm_tm(K, fw, wt, wb, 512, 16, hT, act_bufs, range(nsub), psr, evac_v)
            fw.barrier()


def load_rows_T(K, fw, st, src_dram, nrows, ncols, name):
    nch = ncols // 128
    out = fw.sb(name, [128, nch, nrows], F32, st); ob = Buf(name)
    PC = 16
    with ExitStack() as pst:
        raw = fw.sb(name + "raw", [nrows, PC * 128], F32, pst); rb = Buf(name + "raw")
        pt = fw.ps(name + "ps", [128, 512], F32, pst); pb = Buf(name + "ps")
        for c0 in range(0, nch, PC):
            c1 = min(nch, c0 + PC)
            fw.dma("sp", raw[:, 0:(c1 - c0) * 128], src_dram[:, c0 * 128:c1 * 128], writes=[rb], key=name)
            for c in range(c0, c1):
                fw.op("pe", lambda e: e.transpose(pt[:, (c - c0) * nrows:(c - c0 + 1) * nrows], raw[0:nrows, (c - c0) * 128:(c - c0 + 1) * 128],
                                                  K.cs("ident")[0:nrows, 0:nrows]),
                      reads=[rb, K.cstb], writes=[pb], inc=(c == c1 - 1))
            fw.op("dve", lambda e: e.tensor_copy(out[:, c0:c1, :], pt[:, 0:(c1 - c0) * nrows].rearrange("p (c j) -> p c j", j=nrows)),
                  reads=[pb], writes=[ob])
        fw.barrier()
    return out, ob


def phase_gdn_prep(K, fw, st):
    cw, cwb = load_rows_T(K, fw, st, K.gdn_conv, 4, 3072, "gcw")
    HN = 2048
    xin = Ring(fw, "gxin", 2, [128, 3 + HN], F32, st)
    t1r = Ring(fw, "gt1", 2, [128, HN], F32, st)
    yr = Ring(fw, "gy", 2, [128, HN], F32, st)
    sqr = Ring(fw, "gsq", 2, [128, HN], F32, st)
    rir = Ring(fw, "gri", 2, [128, HN], F32, st)
    ob = Ring(fw, "gob", 2, [128, HN], BF16, st)
    psr = PsRing(fw, st, 8)
    epsl = K.epsc
    for h in range(8):
        for typ in range(3):
            ntot = NQ if typ == 0 else L
            tok0 = Q0 if typ == 0 else 0
            row0 = typ * 1024 + h * 128
            ch = typ * 8 + h
            for hs in range(0, ntot, HN):
                n = min(HN, ntot - hs)
                xt, xb, xi = xin.next()
                if hs == 0:
                    fw.op("pool", lambda e: e.memset(xt[:, 0:3], 0.0), writes=[xb])
                    fw.dma("sp", xt[:, 3:3 + n], K.qkvA[row0:row0 + 128, tok0:tok0 + n], reads=[xb], writes=[xb], key=f"gx{xi}")
                else:
                    fw.dma("sp", xt[:, 0:3 + n], K.qkvA[row0:row0 + 128, tok0 + hs - 3:tok0 + hs + n], writes=[xb], key=f"gx{xi}")
                fw.flush()
                t1, t1b, _ = t1r.next(); y, yb, _ = yr.next()
                fw.op("dve", lambda e: e.tensor_scalar(t1[:, 0:n], xt[:, 0:n], cw[:, ch, 0:1], None, ALU.mult), reads=[xb, cwb], writes=[t1b])
                for j in (1, 2, 3):
                    fw.op("dve", lambda e: e.scalar_tensor_tensor(t1[:, 0:n], xt[:, j:j + n], cw[:, ch, j:j + 1], t1[:, 0:n], ALU.mult, ALU.add),
                          reads=[xb, cwb, t1b], writes=[t1b])
                fw.op("act", lambda e: e.activation(out=y[:, 0:n], in_=t1[:, 0:n], func=AF.Silu), reads=[t1b], writes=[yb])
                ot, otb, oi = ob.next()
                if typ == 2:
                    fw.op("act", lambda e: e.activation(out=ot[:, 0:n], in_=y[:, 0:n], func=AF.Copy), reads=[yb], writes=[otb])
                    dst = K.vsT
                else:
                    sq, sqb, _ = sqr.next(); rinv, rinvb, _ = rir.next()
                    fw.op("act", lambda e: e.activation(out=sq[:, 0:n], in_=y[:, 0:n], func=AF.Square), reads=[yb], writes=[sqb])
                    for t0 in range(0, n, 512):
                        tn = min(512, n - t0)
                        pt, pb = psr.next()
                        fw.op("pe", lambda e: e.matmul(pt[:, 0:tn], K.cs("ones"), sq[:, t0:t0 + tn], start=True, stop=True),
                              reads=[sqb, K.cstb], writes=[pb])
                        fw.op("act", lambda e: e.activation(out=rinv[:, t0:t0 + tn], in_=pt[:, 0:tn], func=AF.Ln, bias=epsl[:, 0:1]),
                              reads=[pb, K.epsb], writes=[rinvb])
                    fw.op("act", lambda e: e.activation(out=rinv[:, 0:n], in_=rinv[:, 0:n], func=AF.Exp, scale=-0.5), reads=[rinvb], writes=[rinvb])
                    sc = (128.0 ** -0.5) if typ == 0 else 1.0
                    fw.op("dve", lambda e: e.scalar_tensor_tensor(ot[:, 0:n], y[:, 0:n], sc, rinv[:, 0:n], ALU.mult, ALU.mult),
                          reads=[yb, rinvb], writes=[otb])
                    dst = K.qnT if typ == 0 else K.knT
                fw.dma("sp", dst[h * 128:(h + 1) * 128, hs:hs + n], ot[:, 0:n], reads=[otb], key=f"go{oi}", defer=True)


def phase_gdn(K, fw, st):
    P = 128
    NCH = L // 128
    QCH0 = Q0 // 128

    def T(name, shape, dt=F32):
        return fw.sb(name, shape, dt, st)

    def bc_h(ap2):
        return ap2.unsqueeze(1).to_broadcast([P, 8, P])

    def bc_s(ap8):
        return ap8.unsqueeze(2).to_broadcast([P, 8, P])

    onec = T("g_onec", [P, 1]); oneb = Buf("onec")
    fw.op("pool", lambda e: e.memset(onec[:], 1.0), writes=[oneb])
    hp = T("g_hp", [P, 24]); hpb = Buf("hp")
    fw.dma("sp", hp[:, 0:8], K.a_log[0:1, :].partition_broadcast(P), writes=[hpb], key="ghp")
    fw.dma("sp", hp[:, 8:16], K.dt_bias[0:1, :].partition_broadcast(P), writes=[hpb], key="ghp")
    fw.op("act", lambda e: e.activation(out=hp[:, 16:24], in_=hp[:, 0:8], func=AF.Exp), reads=[hpb], writes=[hpb])
    fw.op("dve", lambda e: e.tensor_scalar(hp[:, 16:24], hp[:, 16:24], -1.0, None, ALU.mult), reads=[hpb], writes=[hpb])
    gnw = T("g_gnw", [P, P]); gnwb = Buf("gnw")
    fw.dma("sp", gnw[:], K.gdn_norm[0:1, :].partition_broadcast(P), writes=[gnwb], key="ggn")
    S = T("g_S", [P, 8, P]); Sb = T("g_Sb", [P, 8, P], BF16)
    Sbuf = [Buf(f"S{h}") for h in range(8)]; Sbbuf = [Buf(f"Sb{h}") for h in range(8)]
    fw.op("pool", lambda e: e.memset(S[:], 0.0), writes=Sbuf)
    fw.op("pool", lambda e: e.memset(Sb[:], 0.0), writes=Sbbuf)
    big = PsRing(fw, st, 2, (P, 1024), F32, "gbig")
    seq = PsRing(fw, st, 3, (P, 512), F32, "gseq")
    ptr = fw.ps("gtr", [P, 1024], BF16, st); ptrb = Buf("gtr")

    class CB:
        pass
    cbs = []
    for r in range(2):
        c = CB()
        def mk(name, shape, dt=F32, c=c, r=r):
            setattr(c, name, T(f"g{r}_{name}", shape, dt))
            setattr(c, name + "_b", Buf(f"g{r}_{name}"))
        for nm in ["knc", "vsc", "qnc"]:
            mk(nm, [P, 8, P], BF16)
        mk("zc", [P, 1024]); mk("sz", [P, 1024])
        mk("sm", [P, 96])
        mk("dg", [P, 8, P]); mk("RB", [P, 8, P]); mk("ERB", [P, 8, P], BF16); mk("tmp", [P, 8, P])
        mk("tS", [P, 8, P]); mk("tU", [P, 8, P])
        for nm in ["KgT", "QgT", "N", "NT", "AT", "Kd", "X0", "X1", "XT0", "XT1", "PT0", "PT1", "og"]:
            mk(nm, [P, 8, P], BF16)
        mk("Vb", [P, 8, P]); mk("oall", [P, 8, P]); mk("osq", [P, 8, P])
        c.R = T(f"g{r}_R", [P, 8, P], BF16); c.R_b = [Buf(f"R{h}") for h in range(8)]
        c.vn = T(f"g{r}_vn", [P, 8, P], BF16); c.vn_b = [Buf(f"vn{h}") for h in range(8)]
        c.oT = T(f"g{r}_oT", [P, 8, P], BF16); c.oT_b = Buf("oT")
        cbs.append(c)

    ident = K.cs("ident"); ones = K.cs("ones"); tri = K.cs("tri"); mbS = K.cs("mbS"); mbU = K.cs("mbU")
    CS = [K.cstb]

    import os
    chunks = range(NCH)
    if os.environ.get("GDN_CHUNKS"):
        a, b = os.environ["GDN_CHUNKS"].split(":")
        chunks = range(int(a), int(b))
    GSEC = os.environ.get("GDN_SEC", "Z")
    pending = None
    for n in chunks:
        c = cbs[n % 2]
        isq = n >= QCH0
        t0 = n * P
        sm = c.sm; smB = c.sm_b
        fw.dma("sp", c.knc[:], K.knT[:, t0:t0 + P].rearrange("(h d) t -> d h t", d=P), writes=[c.knc_b], key=f"gk{n % 2}")
        fw.dma("sp", c.vsc[:], K.vsT[:, t0:t0 + P].rearrange("(h d) t -> d h t", d=P), writes=[c.vsc_b], key=f"gv{n % 2}")
        if isq:
            q0 = t0 - Q0
            fw.dma("sp", c.qnc[:], K.qnT[:, q0:q0 + P].rearrange("(h d) t -> d h t", d=P), writes=[c.qnc_b], key=f"gq{n % 2}")
            fw.dma("sp", c.zc[:], K.zTM[q0:q0 + P, :], writes=[c.zc_b], key=f"gz{n % 2}")
        fw.flush()
        fw.op("dve", lambda e: e.tensor_tensor(sm[:, 0:8], K.bd[:, n, 8:16], hp[:, 8:16], ALU.add), reads=[K.bdb, hpb], writes=[smB])
        fw.op("act", lambda e: e.activation(out=sm[:, 8:16], in_=sm[:, 0:8], func=AF.Exp), reads=[smB], writes=[smB])
        fw.op("act", lambda e: e.activation(out=sm[:, 8:16], in_=sm[:, 8:16], func=AF.Ln, bias=onec[:, 0:1]), reads=[smB, oneb], writes=[smB])
        fw.op("dve", lambda e: e.tensor_tensor(sm[:, 16:24], sm[:, 8:16], hp[:, 16:24], ALU.mult), reads=[smB, hpb], writes=[smB])
        ps, psb = seq.next()
        fw.op("pe", lambda e: e.matmul(ps[:, 0:8], tri, sm[:, 16:24], start=True, stop=True), reads=CS + [smB], writes=[psb])
        fw.op("dve", lambda e: e.tensor_copy(sm[:, 24:32], ps[:, 0:8]), reads=[psb], writes=[smB])
        fw.op("act", lambda e: e.activation(out=sm[:, 32:40], in_=K.bd[:, n, 0:8], func=AF.Exp, scale=-1.0), reads=[K.bdb], writes=[smB])
        fw.op("dve", lambda e: e.tensor_scalar(sm[:, 32:40], sm[:, 32:40], 1.0, None, ALU.add), reads=[smB], writes=[smB])
        fw.op("dve", lambda e: e.reciprocal(sm[:, 32:40], sm[:, 32:40]), reads=[smB], writes=[smB])
        fw.op("dve", lambda e: e.tensor_scalar(sm[:, 40:48], sm[:, 32:40], -1.0, None, ALU.mult), reads=[smB], writes=[smB])
        gc = sm[:, 24:32]; beta = sm[:, 32:40]; negb = sm[:, 40:48]
        if GSEC <= "B":
            continue
        fw.op("dve", lambda e: e.tensor_tensor(c.dg[:], bc_h(ident), bc_s(gc), ALU.mult), reads=CS + [smB], writes=[c.dg_b])
        pb_, pbb = big.next()
        for j in range(2):
            fw.op("pe", lambda e: e.matmul(pb_[:, j * 512:(j + 1) * 512], ones,
                                           c.dg[:].rearrange("p h t -> p (h t)")[:, j * 512:(j + 1) * 512], start=True, stop=True),
                  reads=CS + [c.dg_b], writes=[pbb], inc=(j == 1))
        for j in range(2):
            fw.op("act", lambda e: e.activation(out=c.RB[:].rearrange("p h t -> p (h t)")[:, j * 512:(j + 1) * 512],
                                                in_=pb_[:, j * 512:(j + 1) * 512], func=AF.Copy), reads=[pbb], writes=[c.RB_b])
        fw.op("act", lambda e: e.activation(out=c.ERB[:], in_=c.RB[:], func=AF.Exp), reads=[c.RB_b], writes=[c.ERB_b])
        fw.op("act", lambda e: e.activation(out=sm[:, 48:56], in_=c.RB[:, :, P - 1], func=AF.Exp), reads=[c.RB_b], writes=[smB])
        fw.op("dve", lambda e: e.tensor_tensor(sm[:, 64:72], c.RB[:, :, P - 1], gc, ALU.subtract), reads=[c.RB_b, smB], writes=[smB])
        fw.op("act", lambda e: e.activation(out=sm[:, 56:64], in_=sm[:, 64:72], func=AF.Exp), reads=[smB], writes=[smB])
        egl = sm[:, 48:56]; eglc = sm[:, 56:64]
        fw.op("dve", lambda e: e.tensor_tensor(c.tmp[:], c.RB[:], bc_s(gc), ALU.subtract), reads=[c.RB_b, smB], writes=[c.tmp_b])
        fw.op("dve", lambda e: e.tensor_tensor(c.tS[:], c.tmp[:], bc_h(mbS), ALU.add), reads=CS + [c.tmp_b], writes=[c.tS_b])
        fw.op("act", lambda e: e.activation(out=c.tS[:], in_=c.tS[:], func=AF.Exp, scale=-1.0), reads=[c.tS_b], writes=[c.tS_b])
        fw.op("dve", lambda e: e.tensor_tensor(c.tS[:], c.tS[:], bc_s(negb), ALU.mult), reads=[c.tS_b, smB], writes=[c.tS_b])
        if isq:
            fw.op("dve", lambda e: e.tensor_tensor(c.tU[:], c.tmp[:], bc_h(mbU), ALU.add), reads=CS + [c.tmp_b], writes=[c.tU_b])
            fw.op("act", lambda e: e.activation(out=c.tU[:], in_=c.tU[:], func=AF.Exp), reads=[c.tU_b], writes=[c.tU_b])
        if GSEC <= "C":
            continue
        fw.op("dve", lambda e: e.tensor_tensor(c.KgT[:], c.knc[:], c.ERB[:], ALU.mult), reads=[c.knc_b, c.ERB_b], writes=[c.KgT_b])
        if isq:
            fw.op("dve", lambda e: e.tensor_tensor(c.QgT[:], c.qnc[:], c.ERB[:], ALU.mult), reads=[c.qnc_b, c.ERB_b], writes=[c.QgT_b])
        if GSEC <= "D":
            continue
        pb_, pbb = big.next()
        for h in range(8):
            fw.op("pe", lambda e: e.matmul(pb_[:, h * P:(h + 1) * P], c.knc[:, h, :], c.knc[:, h, :], start=True, stop=True),
                  reads=[c.knc_b], writes=[pbb], inc=(h == 7))
        fw.op("dve", lambda e: e.tensor_tensor(c.N[:], pb_[:].rearrange("p (h t) -> p h t", h=8), c.tS[:], ALU.mult),
              reads=[pbb, c.tS_b], writes=[c.N_b])
        if isq:
            pb_, pbb = big.next()
            for h in range(8):
                fw.op("pe", lambda e: e.matmul(pb_[:, h * P:(h + 1) * P], c.knc[:, h, :], c.qnc[:, h, :], start=True, stop=True),
                      reads=[c.knc_b, c.qnc_b], writes=[pbb], inc=(h == 7))
            fw.op("dve", lambda e: e.tensor_tensor(c.AT[:], pb_[:].rearrange("p (h t) -> p h t", h=8), c.tU[:], ALU.mult),
                  reads=[pbb, c.tU_b], writes=[c.AT_b])
        if GSEC <= "E":
            continue
        ptr3 = ptr[:].rearrange("p (h t) -> p h t", h=8)
        for h in range(8):
            fw.op("pe", lambda e: e.transpose(ptr[:, h * P:(h + 1) * P], c.knc[:, h, :], K.identb[:]), reads=[c.knc_b, K.cbuf], writes=[ptrb], inc=(h == 7))
        fw.op("dve", lambda e: e.tensor_tensor(c.Kd[:], ptr3, bc_s(eglc), ALU.mult), reads=[ptrb, smB], writes=[c.Kd_b])
        for h in range(8):
            fw.op("pe", lambda e: e.transpose(ptr[:, h * P:(h + 1) * P], c.vsc[:, h, :], K.identb[:]), reads=[c.vsc_b, K.cbuf], writes=[ptrb], inc=(h == 7))
        fw.op("act", lambda e: e.activation(out=c.Vb[:], in_=ptr3, func=AF.Copy), reads=[ptrb], writes=[c.Vb_b])
        if GSEC <= "F":
            continue
        for h in range(8):
            fw.op("pe", lambda e: e.transpose(ptr[:, h * P:(h + 1) * P], c.N[:, h, :], K.identb[:]), reads=[c.N_b, K.cbuf], writes=[ptrb], inc=(h == 7))
        GD = os.environ.get("GDN_DBG", "")
        if "1" not in GD:
            fw.op("dve", lambda e: e.tensor_copy(c.NT[:], ptr3), reads=[ptrb], writes=[c.NT_b])
        if "2" not in GD:
            fw.op("dve", lambda e: e.tensor_tensor(c.PT0[:], ptr3, bc_h(ident), ALU.add), reads=[ptrb] + CS, writes=[c.PT0_b])
        X, Xb, XT, XTb = c.N, c.N_b, c.NT, c.NT_b
        PT, PTb = c.PT0, c.PT0_b
        for lvl in range(1, 7):
            if lvl > int(os.environ.get("GDN_LVL", "6")):
                break
            Xn, Xnb = (c.X0, c.X0_b) if lvl % 2 else (c.X1, c.X1_b)
            XTn, XTnb = (c.XT0, c.XT0_b) if lvl % 2 else (c.XT1, c.XT1_b)
            PTn, PTnb = (c.PT1, c.PT1_b) if lvl % 2 else (c.PT0, c.PT0_b)
            pa, pab = big.next()
            for h in range(8):
                fw.op("pe", lambda e: e.matmul(pa[:, h * P:(h + 1) * P], XT[:, h, :], X[:, h, :], start=True, stop=True),
                      reads=[Xb, XTb], writes=[pab], inc=(h == 7))
            for hh in range(2):
                fw.op("act", lambda e: e.activation(out=Xn[:, 4 * hh:4 * hh + 4, :], in_=pa[:, hh * 512:(hh + 1) * 512].rearrange("p (h t) -> p h t", h=4),
                                                    func=AF.Copy), reads=[pab], writes=[Xnb])
            if lvl < 6:
                pb2, pb2b = big.next()
                for h in range(8):
                    fw.op("pe", lambda e: e.matmul(pb2[:, h * P:(h + 1) * P], X[:, h, :], XT[:, h, :], start=True, stop=True),
                          reads=[Xb, XTb], writes=[pb2b], inc=(h == 7))
                for hh in range(2):
                    fw.op("act", lambda e: e.activation(out=XTn[:, 4 * hh:4 * hh + 4, :], in_=pb2[:, hh * 512:(hh + 1) * 512].rearrange("p (h t) -> p h t", h=4),
                                                        func=AF.Copy), reads=[pb2b], writes=[XTnb])
            pc, pcb = big.next()
            for h in range(8):
                fw.op("pe", lambda e: e.matmul(pc[:, h * P:(h + 1) * P], Xn[:, h, :], PT[:, h, :], start=True, stop=True),
                      reads=[Xnb, PTb], writes=[pcb], inc=(h == 7))
            fw.op("dve", lambda e: e.tensor_tensor(PTn[:], pc[:].rearrange("p (h t) -> p h t", h=8), PT[:], ALU.add), reads=[pcb, PTb], writes=[PTnb])
            X, Xb, XT, XTb, PT, PTb = Xn, Xnb, XTn, XTnb, PTn, PTnb
            if pending is not None:
                next(pending, None)
        if GSEC <= "G":
            continue
        TT, TTb = c.X0, c.X0_b
        fw.op("dve", lambda e: e.tensor_tensor(TT[:], PT[:], bc_s(beta), ALU.mult), reads=[PTb, smB], writes=[TTb])
        if pending is not None:
            for _ in pending:
                pass

        def seq_gen(c=c, n=n, isq=isq, q0=(t0 - Q0), TT=TT, TTb=TTb, sm=sm, smB=smB, egl=egl):
            for g in range(2):
                hs = range(4 * g, 4 * g + 4)
                Sg = [Sbuf[h] for h in hs]; Sbg = [Sbbuf[h] for h in hs]
                gsl = slice(4 * g, 4 * g + 4)
                p1, p1b = seq.next()
                for h in hs:
                    fw.op("pe", lambda e: e.matmul(p1[:, (h % 4) * P:(h % 4 + 1) * P], c.KgT[:, h, :], Sb[:, h, :], start=True, stop=True),
                          reads=[c.KgT_b] + Sbg, writes=[p1b], inc=(h % 4 == 3))
                fw.op("dve", lambda e: e.tensor_tensor(c.R[:, gsl, :], c.Vb[:, gsl, :], p1[:].rearrange("p (h t) -> p h t", h=4), ALU.subtract),
                      reads=[p1b, c.Vb_b], writes=[c.R_b[g]])
                p2, p2b = seq.next()
                for h in hs:
                    fw.op("pe", lambda e: e.matmul(p2[:, (h % 4) * P:(h % 4 + 1) * P], TT[:, h, :], c.R[:, h, :], start=True, stop=True),
                          reads=[TTb, c.R_b[g]], writes=[p2b], inc=(h % 4 == 3))
                fw.op("act", lambda e: e.activation(out=c.vn[:, gsl, :], in_=p2[:].rearrange("p (h t) -> p h t", h=4), func=AF.Copy),
                      reads=[p2b], writes=[c.vn_b[g]])
                yield
                if isq:
                    po, pob = seq.next()
                    for h in hs:
                        sl = slice((h % 4) * P, (h % 4 + 1) * P)
                        fw.op("pe", lambda e: e.matmul(po[:, sl], c.QgT[:, h, :], Sb[:, h, :], start=True, stop=False),
                              reads=[c.QgT_b] + Sbg, writes=[pob], inc=False)
                        fw.op("pe", lambda e: e.matmul(po[:, sl], c.AT[:, h, :], c.vn[:, h, :], start=False, stop=True),
                              reads=[c.AT_b, c.vn_b[g]], writes=[pob], inc=(h % 4 == 3))
                    fw.op("act", lambda e: e.activation(out=c.oall[:, gsl, :], in_=po[:].rearrange("p (h t) -> p h t", h=4), func=AF.Copy),
                          reads=[pob], writes=[c.oall_b])
                    yield
                p3, p3b = seq.next()
                for h in hs:
                    fw.op("pe", lambda e: e.matmul(p3[:, (h % 4) * P:(h % 4 + 1) * P], c.Kd[:, h, :], c.vn[:, h, :], start=True, stop=True),
                          reads=[c.Kd_b, c.vn_b[g]], writes=[p3b], inc=(h % 4 == 3))
                fw.op("dve", lambda e: e.tensor_tensor(S[:, gsl, :], S[:, gsl, :], bc_s(egl)[:, gsl, :], ALU.mult), reads=Sg + [smB], writes=Sg)
                fw.op("dve", lambda e: e.tensor_tensor(S[:, gsl, :], S[:, gsl, :], p3[:].rearrange("p (h t) -> p h t", h=4), ALU.add),
                      reads=Sg + [p3b], writes=Sg)
                fw.op("act", lambda e: e.activation(out=Sb[:, gsl, :], in_=S[:, gsl, :], func=AF.Copy), reads=Sg, writes=Sbg)
                yield
            if isq:
                fw.op("act", lambda e: e.activation(out=c.sz[:], in_=c.zc[:], func=AF.Silu), reads=[c.zc_b], writes=[c.sz_b])
                fw.op("act", lambda e: e.activation(out=c.osq[:], in_=c.oall[:], func=AF.Square), reads=[c.oall_b], writes=[c.osq_b])
                fw.op("dve", lambda e: e.tensor_reduce(sm[:, 72:80], c.osq[:], AX.X, ALU.add), reads=[c.osq_b], writes=[smB])
                fw.op("act", lambda e: e.activation(out=sm[:, 80:88], in_=sm[:, 72:80], func=AF.Ln, scale=1.0 / 128, bias=K.epsc[:, 0:1]), reads=[smB, K.epsb], writes=[smB])
                fw.op("act", lambda e: e.activation(out=sm[:, 80:88], in_=sm[:, 80:88], func=AF.Exp, scale=-0.5), reads=[smB], writes=[smB])
                fw.op("dve", lambda e: e.tensor_tensor(c.osq[:], c.oall[:], bc_s(sm[:, 80:88]), ALU.mult), reads=[c.oall_b, smB], writes=[c.osq_b])
                fw.op("dve", lambda e: e.tensor_tensor(c.osq[:], c.osq[:], bc_h(gnw[:, :]), ALU.mult), reads=[c.osq_b, gnwb], writes=[c.osq_b])
                fw.op("dve", lambda e: e.tensor_tensor(c.og[:], c.osq[:], c.sz[:].rearrange("p (h t) -> p h t", h=8), ALU.mult), reads=[c.osq_b, c.sz_b], writes=[c.og_b])
                for h in range(8):
                    fw.op("pe", lambda e: e.transpose(ptr[:, h * P:(h + 1) * P], c.og[:, h, :], K.identb[:]), reads=[c.og_b, K.cbuf], writes=[ptrb], inc=(h == 7))
                fw.op("dve", lambda e: e.tensor_copy(c.oT[:], ptr3), reads=[ptrb], writes=[c.oT_b])
                fw.dma("sp", K.oaT[:, q0:q0 + P].rearrange("(h d) t -> d h t", d=P), c.oT[:], reads=[c.oT_b], key=f"go{n % 2}", defer=True)
            yield

        pending = seq_gen()
    if pending is not None:
        for _ in pending:
            pass


def phase_moba(K, fw, st):
    P = 128
    PI = float(np.pi)

    def T(name, shape, dt=F32):
        return fw.sb(name, shape, dt, st)
    CS = [K.cstb]
    posi = T("m_posi", [32, L], I32); posb = Buf("posi")
    fw.dma("sp", posi[:], K.pos[0:1, :].partition_broadcast(32), writes=[posb], key="mpos")
    ang = T("m_ang", [32, L]); angb = Buf("ang")
    fw.op("dve", lambda e: e.tensor_copy(ang[:], posi[:]), reads=[posb], writes=[angb])
    fw.op("dve", lambda e: e.tensor_scalar(ang[:], ang[:], K.cs("invf")[0:32, 0:1], None, ALU.mult), reads=[angb] + CS, writes=[angb])
    tabs = []
    wk = T("m_wk", [32, L]); wkb = Buf("wk")
    wk2 = T("m_wk2", [32, L]); wk2b = Buf("wk2")
    for name, shift in (("sin", 0.0), ("cos", PI / 2)):
        tab = T("m_" + name, [32, L]); tb = Buf(name)
        fw.op("dve", lambda e: e.tensor_scalar(wk[:], ang[:], 1.0 / (2 * PI), shift / (2 * PI) + 0.5, ALU.mult, ALU.add), reads=[angb], writes=[wkb])
        fw.op("dve", lambda e: e.tensor_copy(posi[:], wk[:]), reads=[wkb, posb], writes=[posb])
        fw.op("dve", lambda e: e.tensor_copy(wk[:], posi[:]), reads=[posb], writes=[wkb])
        fw.op("dve", lambda e: e.tensor_scalar(wk2[:], ang[:], shift, None, ALU.add), reads=[angb], writes=[wk2b])
        fw.op("dve", lambda e: e.scalar_tensor_tensor(tab[:], wk[:], -2 * PI, wk2[:], ALU.mult, ALU.add), reads=[wkb, wk2b], writes=[tb])
        fw.op("dve", lambda e: e.tensor_scalar(wk[:], tab[:], -PI, 2 * PI, ALU.is_lt, ALU.mult), reads=[tb], writes=[wkb])
        fw.op("dve", lambda e: e.tensor_tensor(tab[:], tab[:], wk[:], ALU.add), reads=[tb, wkb], writes=[tb])
        fw.op("dve", lambda e: e.tensor_scalar(wk[:], tab[:], PI, -2 * PI, ALU.is_gt, ALU.mult), reads=[tb], writes=[wkb])
        fw.op("dve", lambda e: e.tensor_tensor(tab[:], tab[:], wk[:], ALU.add), reads=[tb, wkb], writes=[tb])
        fw.op("dve", lambda e: e.tensor_scalar(tab[:], tab[:], -PI, PI, ALU.max, ALU.min), reads=[tb], writes=[tb])
        fw.op("act", lambda e: e.activation(out=tab[:], in_=tab[:], func=AF.Sin), reads=[tb], writes=[tb])
        tabs.append((tab, tb))
    (sinT, sinb), (cosT, cosb) = tabs
    blk = [7] + [8 + (s - 1) // 2 for s in range(1, 17)]
    cv = T("m_cv", [P, 16]); cvb = Buf("cv")
    fw.dma("sp", cv[:], K.cvalid[0:1, :].partition_broadcast(P), writes=[cvb], key="mcv")
    vb1 = T("m_vb1", [P, 17, 16]); vb2 = T("m_vb2", [P, 17, 16]); nown = T("m_nown", [P, 17, 16]); vbb = Buf("vb")
    fw.op("pool", lambda e: e.memset(vb1[:], 0.0), writes=[vbb])
    fw.op("pool", lambda e: e.memset(vb2[:], 0.0), writes=[vbb])
    fw.op("pool", lambda e: e.memset(nown[:], 1.0), writes=[vbb])
    for s_ in range(17):
        b = blk[s_]
        fw.op("pool", lambda e: e.memset(vb1[:, s_, b:16], -BIG), writes=[vbb])
        if b + 1 < 16:
            fw.op("pool", lambda e: e.memset(vb2[:, s_, b + 1:16], -BIG), writes=[vbb])
        fw.op("pool", lambda e: e.memset(nown[:, s_, b:b + 1], 0.0), writes=[vbb])
    cvbc = cv[:, :].unsqueeze(1).to_broadcast([P, 17, 16])
    fw.op("pool", lambda e: e.tensor_tensor(vb1[:], vb1[:], cvbc, ALU.add), reads=[vbb, cvb], writes=[vbb])
    fw.op("pool", lambda e: e.tensor_tensor(vb2[:], vb2[:], cvbc, ALU.add), reads=[vbb, cvb], writes=[vbb])
    qx = T("m_qx", [P, NQ]); qxb = Buf("qx")
    kx = T("m_kx", [P, L]); kxb = Buf("kx")
    qTb = T("m_qTb", [P, NQ], BF16); qTbb = Buf("qTb")
    kTb = T("m_kTb", [P, L], BF16); kTbb = Buf("kTb")
    Vh = T("m_Vh", [P, 32, P], BF16); Vhb = Buf("Vh")
    km = T("m_km", [P, 16]); kmb_ = T("m_kmb", [P, 16], BF16); kmB = Buf("km")
    gt = T("m_gt", [P, 17, 16]); gtb = Buf("gt")
    mx = T("m_mx", [P, 17, 8]); mxb = Buf("mx")
    sel = T("m_sel", [P, 17, 16]); selb = Buf("sel")
    biasT = T("m_biasT", [16, NQ], BF16); biasTb = Buf("biasT")
    rt1 = Ring(fw, "m_rt1", 2, [32, 512], F32, st)
    pT = Ring(fw, "m_pT", 4, [P, 512], BF16, st)
    rl = T("m_rl", [P, 512]); rlb = Buf("rl")
    obt = Ring(fw, "m_ob", 2, [P, 512], BF16, st)
    psS = PsRing(fw, st, 4, (P, 512), F32, "mS")
    psO = fw.ps("mO", [P, 512], F32, st); psOb = Buf("mO")
    psL = fw.ps("mL", [P, 512], F32, st); psLb = Buf("mL")
    pm = K.cs("pm")[0:32, :]

    for h in range(8):
        fw.dma("sp", qx[:], K.qkB[h * P:(h + 1) * P, Q0:L], writes=[qxb], key="mq")
        fw.dma("sp", kx[:], K.qkB[1024 + h * P:1024 + (h + 1) * P, :], writes=[kxb], key="mk")
        fw.dma("sp", Vh[:], K.vB[:, h * P:(h + 1) * P].rearrange("(s p) d -> p s d", p=P), writes=[Vhb], key="mv")
        fw.flush()
        for (xt, xb_, n, tok0) in ((qx, qxb, NQ, Q0), (kx, kxb, L, 0)):
            for t0 in range(0, n, 512):
                tn = min(512, n - t0)
                ps, psb = psS.next()
                fw.op("pe", lambda e: e.matmul(ps[0:32, 0:tn], pm, xt[0:32, t0:t0 + tn], start=True, stop=True), reads=CS + [xb_], writes=[psb])
                r1, r1b, _ = rt1.next()
                fw.op("dve", lambda e: e.tensor_tensor(r1[:, 0:tn], ps[0:32, 0:tn], sinT[:, tok0 + t0:tok0 + t0 + tn], ALU.mult), reads=[psb, sinb], writes=[r1b])
                fw.op("dve", lambda e: e.tensor_tensor(xt[0:32, t0:t0 + tn], xt[0:32, t0:t0 + tn], cosT[:, tok0 + t0:tok0 + t0 + tn], ALU.mult),
                      reads=[xb_, cosb], writes=[xb_])
                fw.op("dve", lambda e: e.tensor_tensor(xt[0:32, t0:t0 + tn], xt[0:32, t0:t0 + tn], r1[:, 0:tn], ALU.add), reads=[xb_, r1b], writes=[xb_])
        fw.op("act", lambda e: e.activation(out=qTb[:], in_=qx[:], func=AF.Copy, scale=128.0 ** -0.5), reads=[qxb], writes=[qTbb])
        fw.op("act", lambda e: e.activation(out=kTb[:], in_=kx[:], func=AF.Copy), reads=[kxb], writes=[kTbb])
        fw.op("dve", lambda e: e.tensor_reduce(km[:], kx[:].rearrange("p (n b) -> p n b", b=256), AX.X, ALU.add), reads=[kxb], writes=[kmB])
        fw.op("dve", lambda e: e.tensor_scalar(kmb_[:], km[:], 1.0 / 256, None, ALU.mult), reads=[kmB], writes=[kmB])
        ps, psb = psS.next()
        for s_ in range(17):
            fw.op("pe", lambda e: e.matmul(ps[:, s_ * 16:(s_ + 1) * 16], qTb[:, s_ * P:(s_ + 1) * P], kmb_[:], start=True, stop=True),
                  reads=[qTbb, kmB], writes=[psb], inc=(s_ == 16))
        fw.op("dve", lambda e: e.tensor_tensor(gt[:], ps[:, 0:272].rearrange("p (s n) -> p s n", n=16), vb1[:], ALU.add), reads=[psb, vbb], writes=[gtb])
        for s_ in range(17):
            fw.op("dve", lambda e: e.max(mx[:, s_, :], gt[:, s_, :]), reads=[gtb], writes=[mxb])
        fw.op("dve", lambda e: e.tensor_tensor(sel[:], gt[:], mx[:, :, 2:3].to_broadcast([P, 17, 16]), ALU.is_ge), reads=[gtb, mxb], writes=[selb])
        fw.op("dve", lambda e: e.tensor_scalar(sel[:], sel[:], BIG, -BIG, ALU.mult, ALU.add), reads=[selb], writes=[selb])
        fw.op("dve", lambda e: e.tensor_tensor(sel[:], sel[:], vb2[:], ALU.add), reads=[selb, vbb], writes=[selb])
        fw.op("dve", lambda e: e.tensor_tensor(sel[:], sel[:], nown[:], ALU.mult), reads=[selb, vbb], writes=[selb])
        for g0 in range(0, 17, 4):
            g1 = min(17, g0 + 4)
            ps, psb = psS.next()
            for s_ in range(g0, g1):
                fw.op("pe", lambda e: e.transpose(ps[0:16, (s_ - g0) * P:(s_ - g0 + 1) * P], sel[:, s_, :], K.cs("ident")),
                      reads=[selb] + CS, writes=[psb], inc=(s_ == g1 - 1))
            fw.op("act", lambda e: e.activation(out=biasT[:, g0 * P:g1 * P], in_=ps[0:16, 0:(g1 - g0) * P], func=AF.Copy), reads=[psb], writes=[biasTb])
        for (t0, tn) in QTILES:
            kend = (Q0 + t0 + tn) // P
            pend = []

            def flush_one():
                ks_, pt_, ptb_ = pend.pop(0)
                fw.op("pe", lambda e: e.matmul(psO[:, 0:tn], Vh[:, ks_, :], pt_[:, 0:tn], start=(ks_ == 0), stop=(ks_ == kend - 1)),
                      reads=[Vhb, ptb_], writes=[psOb], inc=False)
                fw.op("pe", lambda e: e.matmul(psL[:, 0:tn], K.onesb[:], pt_[:, 0:tn], start=(ks_ == 0), stop=(ks_ == kend - 1)),
                      reads=[K.cbuf, ptb_], writes=[psLb], inc=True)

            for ks in range(kend):
                n = ks // 2
                diag = ks * P + P - 1 >= Q0 + t0
                ps, psb = psS.next()
                fw.op("pe", lambda e: e.matmul(ps[:, 0:tn], kTb[:, ks * P:(ks + 1) * P], qTb[:, t0:t0 + tn], start=True, stop=False),
                      reads=[kTbb, qTbb], writes=[psb], inc=False)
                fw.op("pe", lambda e: e.matmul(ps[:, 0:tn], K.Eb[0:16, n * P:(n + 1) * P], biasT[:, t0:t0 + tn], start=False, stop=not diag),
                      reads=[K.cbuf, biasTb], writes=[psb], inc=not diag)
                if diag:
                    o = (ks * P - (Q0 + t0)) // P
                    fw.op("pe", lambda e: e.matmul(ps[:, 0:tn], K.identb[:], K.cbb[:, o * 512:o * 512 + tn], start=False, stop=True),
                          reads=[K.cbuf], writes=[psb])
                pt_, ptb_, _ = pT.next()
                fw.op("act", lambda e: e.activation(out=pt_[:, 0:tn], in_=ps[:, 0:tn], func=AF.Exp), reads=[psb], writes=[ptb_])
                pend.append((ks, pt_, ptb_))
                if len(pend) > 2:
                    flush_one()
            while pend:
                flush_one()
            fw.op("dve", lambda e: e.reciprocal(rl[:, 0:tn], psL[:, 0:tn]), reads=[psLb], writes=[rlb])
            ot, otb, oi = obt.next()
            fw.op("dve", lambda e: e.tensor_tensor(ot[:, 0:tn], psO[:, 0:tn], rl[:, 0:tn], ALU.mult), reads=[psOb, rlb], writes=[otb])
            fw.dma("sp", K.obT[h * P:(h + 1) * P, t0:t0 + tn], ot[:, 0:tn], reads=[otb], key=f"mo{oi}", defer=True)


def phase_merge(K, fw, st):
    P = 128
    oa = fw.sb("mg_oa", [P, 8, NQ], BF16, st); oab = Buf("oa")
    ob = fw.sb("mg_ob", [P, 8, NQ], BF16, st); obb = Buf("ob")
    for kc in range(8):
        fw.dma("sp", oa[:, kc, :], K.oaT[kc * P:(kc + 1) * P, :], writes=[oab] if kc == 0 else [], key="mgoa")
        fw.dma("sp", ob[:, kc, :], K.obT[kc * P:(kc + 1) * P, :], writes=[obb] if kc == 0 else [], key="mgob")
    oab.w = ("mgoa", fw.dcnt["mgoa"]); obb.w = ("mgob", fw.dcnt["mgob"])
    wra = Ring(fw, "mg_wa", 2, [P, 8, 512], BF16, st)
    wrb = Ring(fw, "mg_wb", 2, [P, 8, 512], BF16, st)
    gr = Ring(fw, "mg_g", 4, [P, 512], F32, st)
    m1 = Ring(fw, "mg_m1", 2, [P, 512], F32, st)
    m2 = Ring(fw, "mg_m2", 2, [P, 512], F32, st)
    mo = Ring(fw, "mg_mo", 2, [P, 512], BF16, st)
    psr = PsRing(fw, st, 8)
    for c0 in range(0, D, 512):
        wa, wab = load_wslab(K, fw, wra, K.w_a, c0, 512, 8, "mgwa")
        wb, wbb = load_wslab(K, fw, wrb, K.w_b, c0, 512, 8, "mgwb")
        for j in range(4):
            r0 = c0 + j * P
            for (t0, tn) in QTILES:
                ga, gab, gi = gr.next()
                fw.dma("sp", ga[:, 0:tn], K.gA[r0:r0 + P, t0:t0 + tn], writes=[gab], key=f"mgg{gi}")
                gb, gbb, gi = gr.next()
                fw.dma("sp", gb[:, 0:tn], K.gB[r0:r0 + P, t0:t0 + tn], writes=[gbb], key=f"mgg{gi}")
                fw.flush()
                pa, pab = psr.next()
                for kc in range(8):
                    fw.op("pe", lambda e: e.matmul(pa[:, 0:tn], wa[:, kc, j * P:(j + 1) * P], oa[:, kc, t0:t0 + tn], start=(kc == 0), stop=(kc == 7)),
                          reads=[wab, oab], writes=[pab], inc=(kc == 7))
                pb, pbb = psr.next()
                for kc in range(8):
                    fw.op("pe", lambda e: e.matmul(pb[:, 0:tn], wb[:, kc, j * P:(j + 1) * P], ob[:, kc, t0:t0 + tn], start=(kc == 0), stop=(kc == 7)),
                          reads=[wbb, obb], writes=[pbb], inc=(kc == 7))
                a1, a1b, _ = m1.next(); a2, a2b, _ = m2.next(); o_, o_b, oi = mo.next()
                fw.op("dve", lambda e: e.tensor_tensor(a1[:, 0:tn], pa[:, 0:tn], ga[:, 0:tn], ALU.mult), reads=[pab, gab], writes=[a1b])
                fw.op("dve", lambda e: e.tensor_tensor(a2[:, 0:tn], pb[:, 0:tn], gb[:, 0:tn], ALU.mult), reads=[pbb, gbb], writes=[a2b])
                fw.op("dve", lambda e: e.tensor_tensor(o_[:, 0:tn], a1[:, 0:tn], a2[:, 0:tn], ALU.add), reads=[a1b, a2b], writes=[o_b])
                fw.dma("sp", K.mT[t0 // P:(t0 + tn) // P, :, r0 // P, :].rearrange("s p t -> p s t"),
                       o_[:, 0:tn].rearrange("p (s t) -> p s t", t=P), reads=[o_b], key=f"mgo{oi}", defer=True)


def phase_outproj(K, fw, st):
    P = 128
    wr = Ring(fw, "op_w", 2, [P, 16, 512], BF16, st)
    ms = Ring(fw, "op_m", 3, [P, 16, P], BF16, st)
    xr = Ring(fw, "op_x", 3, [P, 512], F32, st)
    orr = Ring(fw, "op_o", 3, [P, 512], F32, st)
    psr = PsRing(fw, st, 6)
    for g in range(4):
        wt, wb = load_wslab(K, fw, wr, K.w_out, g * 512, 512, 16, "opw")
        for s_ in range(17):
            mt, mb, mi = ms.next()
            fw.dma("sp", mt[:], K.mT[s_, :, :, :], writes=[mb], key=f"opm{mi}")
            xt, xb, xi = xr.next()
            fw.dma("sp", xt[:], K.x[Q0 + s_ * P:Q0 + (s_ + 1) * P, g * 512:(g + 1) * 512], writes=[xb], key=f"opx{xi}")
            fw.flush()
            pt, pb = psr.next()
            for kc in range(16):
                fw.op("pe", lambda e: e.matmul(pt[:, :], mt[:, kc, :], wt[:, kc, :], start=(kc == 0), stop=(kc == 15)),
                      reads=[mb, wb], writes=[pb], inc=(kc == 15))
            ot, ob, oi = orr.next()
            fw.op("dve", lambda e: e.tensor_tensor(ot[:], pt[:, :], xt[:], ALU.add), reads=[pb, xb], writes=[ob])
            fw.dma("sp", K.x2[s_ * P:(s_ + 1) * P, g * 512:(g + 1) * 512], ot[:], reads=[ob], key=f"opo{oi}", defer=True)


def phase_ffn_up(K, fw, st):
    P = 128
    h2 = fw.sb("fu_h2", [P, 16, NQ], BF16, st)
    h2b = [Buf(f"h2{i}") for i in range(17)]
    with ExitStack() as pst:
        pp = [(fw.ps(f"fptr{i}", [P, 2048], BF16, pst), Buf(f"fptr{i}")) for i in range(2)]
        norm_to_hT(K, fw, pst, K.x2, [i * P for i in range(17)], K.ln2, h2, h2b, pp, "n2")
        fw.barrier()
    fcw, fcwb = load_rows_T(K, fw, st, K.ffn_conv, 3, 2 * DFF, "fcw")
    fcb, fcbb = load_rows_T(K, fw, st, K.ffn_bias, 1, 2 * DFF, "fcb")
    wr = Ring(fw, "fu_w", 4, [P, 16, 256], BF16, st)
    ug = Ring(fw, "fu_ug", 2, [P, 2 + NQ], F32, st)
    uv = Ring(fw, "fu_uv", 2, [P, 2 + NQ], F32, st)
    for r in (ug, uv):
        for t, b_ in zip(r.t, r.b):
            fw.op("pool", lambda e: e.memset(t[:, 0:2], 0.0), writes=[b_])
    cgr = Ring(fw, "fu_cg", 2, [P, NQ], F32, st)
    cvr = Ring(fw, "fu_cv", 2, [P, NQ], F32, st)
    uo = Ring(fw, "fu_uo", 2, [P, NQ], BF16, st)
    psr = PsRing(fw, st, 8)

    def act_bufs(t0, tn):
        return h2b[t0 // P:(t0 + tn + P - 1) // P]

    for j2 in range(0, 44, 2):
        wg, wgb = load_wslab(K, fw, wr, K.w_up, j2 * P, 256, 16, "fuw")
        wv, wvb = load_wslab(K, fw, wr, K.w_up, DFF + j2 * P, 256, 16, "fuw")
        for jj in range(2):
            j = j2 + jj
            ugt, ugb, _ = ug.next(); uvt, uvb, _ = uv.next()
            for (wt, wb, ut, ub) in ((wg, wgb, ugt, ugb), (wv, wvb, uvt, uvb)):
                for (t0, tn) in QTILES:
                    pt, pb = psr.next()
                    for kc in range(16):
                        fw.op("pe", lambda e: e.matmul(pt[:, 0:tn], wt[:, kc, jj * P:(jj + 1) * P], h2[:, kc, t0:t0 + tn], start=(kc == 0), stop=(kc == 15)),
                              reads=[wb] + act_bufs(t0, tn), writes=[pb], inc=(kc == 15))
                    fw.op("act", lambda e: e.activation(out=ut[:, 2 + t0:2 + t0 + tn], in_=pt[:, 0:tn], func=AF.Copy), reads=[pb], writes=[ub])
            cg, cgb, _ = cgr.next(); cv, cvb, _ = cvr.next()
            for (ut, ub, ct, cb_, ch) in ((ugt, ugb, cg, cgb, j), (uvt, uvb, cv, cvb, 44 + j)):
                fw.op("dve", lambda e: e.tensor_scalar(ct[:], ut[:, 0:NQ], fcw[:, ch, 0:1], fcb[:, ch, 0:1], ALU.mult, ALU.add),
                      reads=[ub, fcwb, fcbb], writes=[cb_])
                for jt in (1, 2):
                    fw.op("dve", lambda e: e.scalar_tensor_tensor(ct[:], ut[:, jt:NQ + jt], fcw[:, ch, jt:jt + 1], ct[:], ALU.mult, ALU.add),
                          reads=[ub, fcwb, cb_], writes=[cb_])
            fw.op("act", lambda e: e.activation(out=cg[:], in_=cg[:], func=AF.Silu), reads=[cgb], writes=[cgb])
            ot, otb, oi = uo.next()
            fw.op("dve", lambda e: e.tensor_tensor(ot[:], cg[:], cv[:], ALU.mult), reads=[cgb, cvb], writes=[otb])
            fw.dma("sp", K.uT[:, :, j, :].rearrange("s p t -> p s t"), ot[:, :].rearrange("p (s t) -> p s t", t=P), reads=[otb], key=f"fuo{oi}")


def phase_ffn_down(K, fw, st):
    P = 128
    NKC = DFF // P
    wr = Ring(fw, "fd_w", 2, [P, NKC, 512], BF16, st)
    us = Ring(fw, "fd_u", 2, [P, NKC, P], BF16, st)
    xr = Ring(fw, "fd_x", 3, [P, 512], F32, st)
    orr = Ring(fw, "fd_o", 3, [P, 512], F32, st)
    psr = PsRing(fw, st, 6)
    for g in range(4):
        wt, wb = load_wslab(K, fw, wr, K.w_down, g * 512, 512, NKC, "fdw")
        for s_ in range(16):
            ut, ub, ui = us.next()
            fw.dma("sp", ut[:], K.uT[1 + s_, :, :, :], writes=[ub], key=f"fdu{ui}")
            xt, xb, xi = xr.next()
            fw.dma("sp", xt[:], K.x2[P + s_ * P:P + (s_ + 1) * P, g * 512:(g + 1) * 512], writes=[xb], key=f"fdx{xi}")
            fw.flush()
            pt, pb = psr.next()
            for kc in range(NKC):
                fw.op("pe", lambda e: e.matmul(pt[:, :], ut[:, kc, :], wt[:, kc, :], start=(kc == 0), stop=(kc == NKC - 1)),
                      reads=[ub, wb], writes=[pb], inc=(kc == NKC - 1))
            ot, ob, oi = orr.next()
            fw.op("dve", lambda e: e.tensor_tensor(ot[:], pt[:, :], xt[:], ALU.add), reads=[pb, xb], writes=[ob])
            fw.dma("sp", K.x3[s_ * P:(s_ + 1) * P, g * 512:(g + 1) * 512], ot[:], reads=[ob], key=f"fdo{oi}", defer=True)


def phase_final(K, fw, st):
    P = 128
    fn = fw.sb("fn_w", [P, D], F32, st); fnb = Buf("fnw")
    fw.dma("sp", fn[:], K.fnorm[0:1, :].partition_broadcast(P), writes=[fnb], key="fnw")
    xr = Ring(fw, "fn_x", 2, [P, D], F32, st)
    yr = Ring(fw, "fn_y", 2, [P, D], F32, st)
    junk = fw.sb("fn_junk", [P, D], BF16, st); junkb = Buf("junk")
    ss = Ring(fw, "fn_ss", 2, [P, 4], F32, st)
    for s_ in range(16):
        xt, xb, xi = xr.next()
        fw.dma("sp", xt[:], K.x3[s_ * P:(s_ + 1) * P, :], writes=[xb], key=f"fnx{xi}")
        fw.flush()
        sst, ssb, _ = ss.next()
        fw.op("act", lambda e: e.activation(out=junk[:], in_=xt[:], func=AF.Square, accum_out=sst[:, 0:1]), reads=[xb], writes=[junkb, ssb])
        fw.op("act", lambda e: e.activation(out=sst[:, 1:2], in_=sst[:, 0:1], func=AF.Ln, scale=1.0 / D, bias=K.epsc[:, 0:1]),
              reads=[ssb, K.epsb], writes=[ssb])
        fw.op("act", lambda e: e.activation(out=sst[:, 2:3], in_=sst[:, 1:2], func=AF.Exp, scale=-0.5), reads=[ssb], writes=[ssb])
        yt, yb, yi = yr.next()
        fw.op("dve", lambda e: e.scalar_tensor_tensor(yt[:], xt[:], sst[:, 2:3], fn[:], ALU.mult, ALU.mult), reads=[xb, ssb, fnb], writes=[yb])
        fw.dma("sp", K.out[s_ * P:(s_ + 1) * P, :], yt[:], reads=[yb], key=f"fno{yi}", defer=True)


def make_in_maps(inputs, cores=range(8)):
    x = np.asarray(inputs["x"], np.float32)
    pos = np.asarray(inputs["positions"], np.int32)
    shared = {
        "consts": CONST_ARR,
        "ln1": np.asarray(inputs["ln1"], np.float32).reshape(1, D),
        "ln2": np.asarray(inputs["ln2"], np.float32).reshape(1, D),
        "final_norm": np.asarray(inputs["final_norm"], np.float32).reshape(1, D),
        "w_in": np.ascontiguousarray(np.asarray(inputs["w_in"], np.float32)[0]),
        "gdn_conv": np.ascontiguousarray(np.asarray(inputs["gdn_conv"], np.float32)[0]),
        "gdn_a_log": np.asarray(inputs["gdn_a_log"], np.float32).reshape(1, 8),
        "gdn_dt_bias": np.asarray(inputs["gdn_dt_bias"], np.float32).reshape(1, 8),
        "gdn_norm": np.asarray(inputs["gdn_norm"], np.float32).reshape(1, 128),
        "w_branch_a": np.ascontiguousarray(np.asarray(inputs["w_branch_a"], np.float32)[0]),
        "w_branch_b": np.ascontiguousarray(np.asarray(inputs["w_branch_b"], np.float32)[0]),
        "w_out": np.ascontiguousarray(np.asarray(inputs["w_out"], np.float32)[0]),
        "w_up": np.ascontiguousarray(np.asarray(inputs["w_up"], np.float32)[0]),
        "ffn_conv": np.ascontiguousarray(np.asarray(inputs["ffn_conv"], np.float32)[0]),
        "ffn_conv_bias": np.asarray(inputs["ffn_conv_bias"], np.float32).reshape(1, 2 * DFF),
        "w_down": np.ascontiguousarray(np.asarray(inputs["w_down"], np.float32)[0]),
    }
    maps = []
    for c in cores:
        b, half = c // 2, c % 2
        m = dict(shared)
        if half == 1:
            m["x"] = np.ascontiguousarray(x[b])
            m["pos"] = np.ascontiguousarray(pos[b]).reshape(1, L)
            m["cvalid"] = np.zeros((1, 16), np.float32)
        else:
            xl = np.zeros((L, D), np.float32)
            xl[2048:] = x[b, :2048]
            pl = np.zeros((1, L), np.int32)
            pl[0, 2048:] = pos[b, :2048]
            m["x"] = xl
            m["pos"] = pl
            cv = np.zeros((1, 16), np.float32)
            cv[0, :8] = -BIG
            m["cvalid"] = cv
        maps.append(m)
    return maps


_NC_CACHE = {}


def kernel(**inputs):
    if "nc" not in _NC_CACHE:
        _NC_CACHE["nc"] = build_program()
    nc = _NC_CACHE["nc"]
    maps = make_in_maps(inputs)
    res = run_bass_kernel_spmd(nc, maps, core_ids=list(range(8)))
    out = np.zeros((4, 4096, D), np.float32)
    for c in range(8):
        b, half = c // 2, c % 2
        out[b, half * 2048:(half + 1) * 2048] = res.results[c]["out"]
    return out
```

```python
import numpy as np
from contextlib import ExitStack
import concourse.bass as bass
import concourse.mybir as mybir
from concourse.bass_utils import run_bass_kernel_spmd

F32 = mybir.dt.float32
BF16 = mybir.dt.bfloat16
I32 = mybir.dt.int32
AF = mybir.ActivationFunctionType
ALU = mybir.AluOpType
AX = mybir.AxisListType

import os as _os
SAME_ENGINE_SYNC = _os.environ.get("SAME_ENGINE_SYNC", "1") == "1"

D = 2048
L = 4096
Q0 = 1920
NQ = 2176
IN_DIM = 11280
DFF = 5632
BIG = 1.0e30
EPS = 1e-6
C_QKVA, C_BETA, C_DEC, C_Z, C_QKVB, C_GA, C_GB = 0, 3072, 3080, 3088, 4112, 7184, 9232

QTILES = [(0, 128), (128, 512), (640, 512), (1152, 512), (1664, 512)]
CTXTILES = [(0, 512), (512, 512), (1024, 512), (1536, 384)]


class Buf:
    __slots__ = ("name", "w", "r")

    def __init__(self, name=""):
        self.name = name
        self.w = None
        self.r = []


class FW:
    def __init__(self, nc, stack):
        self.nc = nc
        self.stack = stack
        self.eng = {"pe": nc.tensor, "act": nc.scalar, "dve": nc.vector, "pool": nc.gpsimd, "sp": nc.sync}
        self.sem, self.cnt, self.waited = {}, {}, {}
        for k in self.eng:
            self.sem[k] = stack.enter_context(nc.semaphore("s_" + k))
            self.cnt[k] = 0
            self.waited[k] = {}
        self.dsem, self.dcnt = {}, {}
        self.nwait = 0
        self.ninst = 0
        self.uid = 0
        self.deferred = []

    def sb(self, name, shape, dt=F32, stack=None):
        self.uid += 1
        return (stack or self.stack).enter_context(self.nc.sbuf_tensor(f"{name}_{self.uid}", list(shape), dt))

    def ps(self, name, shape, dt=F32, stack=None):
        self.uid += 1
        return (stack or self.stack).enter_context(self.nc.psum_tensor(f"{name}_{self.uid}", list(shape), dt))

    def dma_sem(self, key):
        if key not in self.dsem:
            self.dsem[key] = self.stack.enter_context(self.nc.semaphore("d_" + key))
            self.dcnt[key] = 0
        return key

    def _semh(self, key):
        return self.sem[key] if key in self.sem else self.dsem[key]

    def _wait(self, ek, k, v):
        if self.waited[ek].get(k, 0) >= v:
            return
        self.eng[ek].wait_ge(self._semh(k), v)
        self.nwait += 1
        self.waited[ek][k] = v

    def _deps(self, ek, reads, writes):
        deps = {}
        for b in reads:
            if b.w is not None:
                k, v = b.w
                deps[k] = max(deps.get(k, 0), v)
        for b in writes:
            if b.w is not None:
                k, v = b.w
                deps[k] = max(deps.get(k, 0), v)
            for (k, v) in b.r:
                deps[k] = max(deps.get(k, 0), v)
        for k, v in deps.items():
            if k in self.dcnt:
                v = self.dcnt[k]
            if k == ek:
                if (not SAME_ENGINE_SYNC) or ek in ("pe", "sp") or v > self.cnt[ek]:
                    continue
            self._wait(ek, k, v)

    def flush(self):
        d, self.deferred = self.deferred, []
        for (qk, out, in_, reads, writes, key, kw) in d:
            self.dma(qk, out, in_, reads=reads, writes=writes, key=key, **kw)

    def _guard(self, writes):
        if self.deferred:
            for d in self.deferred:
                if any(b is w for b in d[3] for w in writes):
                    self.flush()
                    return

    def op(self, ek, fn, reads=(), writes=(), inc=True):
        self._guard(writes)
        self._deps(ek, reads, writes)
        ins = fn(self.eng[ek])
        self.ninst += 1
        if inc:
            ins.then_inc(self.sem[ek], 1)
            self.cnt[ek] += 1
            tag = (ek, self.cnt[ek])
        else:
            tag = (ek, self.cnt[ek] + 1)
        for b in writes:
            b.w = tag
            b.r = []
        for b in reads:
            b.r.append(tag)
            if len(b.r) > 64:
                b.r = self._compact(b.r)
        return ins

    @staticmethod
    def _compact(r):
        m = {}
        for k, v in r:
            m[k] = max(m.get(k, 0), v)
        return list(m.items())

    def dma(self, qk, out, in_, reads=(), writes=(), key=None, defer=False, **kw):
        if defer:
            self.deferred.append((qk, out, in_, reads, writes, key, kw))
            return None
        self._guard(writes)
        self.dma_sem(key)
        self._deps(qk, reads, writes)
        ins = self.eng[qk].dma_start(out=out, in_=in_, **kw)
        self.ninst += 1
        ins.then_inc(self.dsem[key], 16)
        self.dcnt[key] += 16
        tag = (key, self.dcnt[key])
        for b in writes:
            b.w = tag
            b.r = []
        for b in reads:
            b.r.append(tag)
            if len(b.r) > 64:
                b.r = self._compact(b.r)
        return ins

    def barrier(self):
        self.flush()
        for ek in self.eng:
            for k in self.eng:
                if k != ek and self.cnt[k] > 0:
                    self._wait(ek, k, self.cnt[k])
            for k, v in self.dcnt.items():
                if v > 0:
                    self._wait(ek, k, v)


class Ring:
    def __init__(self, fw, name, n, shape, dt, stack):
        self.t = [fw.sb(f"{name}{i}", shape, dt, stack) for i in range(n)]
        self.b = [Buf(f"{name}{i}") for i in range(n)]
        self.i = 0
        self.n = n

    def next(self):
        i = self.i
        self.i = (i + 1) % self.n
        return self.t[i], self.b[i], i


def build_consts():
    c = {}
    p = np.arange(128)
    eye = np.eye(128, dtype=np.float32)
    c["ident"] = eye
    c["ones"] = np.ones((128, 128), np.float32)
    c["tri"] = (p[:, None] <= p[None, :]).astype(np.float32)
    c["mbS"] = np.where(p[None, :] < p[:, None], 0.0, BIG).astype(np.float32)
    c["mbU"] = np.where(p[None, :] >= p[:, None], 0.0, -BIG).astype(np.float32)
    q = np.arange(512)
    cb = [np.where(p[:, None] + 128 * o <= q[None, :], 0.0, -BIG).astype(np.float32) for o in range(4)]
    c["cb"] = np.concatenate(cb, axis=1)
    pm = np.zeros((32, 32), np.float32)
    for i in range(16):
        pm[i + 16, i] = -1.0
        pm[i, i + 16] = 1.0
    c["pm"] = np.pad(pm, ((0, 96), (0, 0)))
    half = 16
    inv = (500000.0 ** (-np.arange(half, dtype=np.float32) / half)).astype(np.float32)
    invp = np.zeros((128, 1), np.float32)
    invp[:32, 0] = np.concatenate([inv, inv])
    c["invf"] = invp
    E = np.zeros((16, 16, 128), np.float32)
    for n in range(16):
        E[n, n, :] = 1.0
    c["E"] = np.pad(E.reshape(16, 2048), ((0, 112), (0, 0)))
    names = ["ident", "ones", "tri", "mbS", "mbU", "cb", "pm", "invf", "E"]
    offs, cols = {}, 0
    for n in names:
        offs[n] = (cols, c[n].shape[1])
        cols += c[n].shape[1]
    arr = np.concatenate([c[n] for n in names], axis=1).astype(np.float32)
    return arr, offs


CONST_ARR, CONST_OFFS = build_consts()


class Ctx:
    pass


def build_program(debug=None):
    debug = debug or set()
    stop = [d[5:] for d in debug if d.startswith("stop:")]
    stop = stop[0] if stop else None
    nc = bass.Bass("TRN2", target_bir_lowering=False)
    K = Ctx()
    K.nc = nc

    def din(name, shape, dt=F32):
        return nc.dram_tensor(name, list(shape), dt, kind="ExternalInput").ap()

    def dscr(name, shape, dt=F32):
        kind = "ExternalOutput" if name in debug else "Internal"
        return nc.dram_tensor(name, list(shape), dt, kind=kind).ap()

    K.x = din("x", [L, D])
    K.pos = din("pos", [1, L], I32)
    K.cvalid = din("cvalid", [1, 16])
    K.consts = din("consts", list(CONST_ARR.shape))
    K.ln1 = din("ln1", [1, D]); K.ln2 = din("ln2", [1, D]); K.fnorm = din("final_norm", [1, D])
    K.w_in = din("w_in", [D, IN_DIM])
    K.gdn_conv = din("gdn_conv", [4, 3072])
    K.a_log = din("gdn_a_log", [1, 8]); K.dt_bias = din("gdn_dt_bias", [1, 8]); K.gdn_norm = din("gdn_norm", [1, 128])
    K.w_a = din("w_branch_a", [1024, D]); K.w_b = din("w_branch_b", [1024, D]); K.w_out = din("w_out", [D, D])
    K.w_up = din("w_up", [D, 2 * DFF]); K.ffn_conv = din("ffn_conv", [3, 2 * DFF]); K.ffn_bias = din("ffn_conv_bias", [1, 2 * DFF])
    K.w_down = din("w_down", [DFF, D])
    K.out = nc.dram_tensor("out", [2048, D], F32, kind="ExternalOutput").ap()
    K.qkvA = dscr("qkvA", [3072, L])
    K.zTM = dscr("zTM", [NQ, 1024])
    K.qkB = dscr("qkB", [2048, L])
    K.vB = dscr("vB", [L, 1024], BF16)
    K.gA = dscr("gA", [D, NQ]); K.gB = dscr("gB", [D, NQ])
    K.qnT = dscr("qnT", [1024, NQ], BF16); K.knT = dscr("knT", [1024, L], BF16); K.vsT = dscr("vsT", [1024, L], BF16)
    K.oaT = dscr("oaT", [1024, NQ], BF16); K.obT = dscr("obT", [1024, NQ], BF16)
    K.mT = dscr("mT", [17, 128, 16, 128], BF16)
    K.x2 = dscr("x2", [NQ, D])
    K.uT = dscr("uT", [17, 128, 44, 128], BF16)
    K.x3 = dscr("x3", [2048, D])
    K.dbg = dscr("dbg", [128, 4096])
    K.B = {n: Buf(n) for n in ["qkvA", "zTM", "qkB", "vB", "gA", "gB", "qnT", "knT", "vsT", "oaT", "obT", "mT", "x2", "uT", "x3", "out", "dbg"]}

    import os
    with ExitStack() as gst:
        fw = FW(nc, gst)
        K.fw = fw
        cw = CONST_ARR.shape[1]
        K.cst = fw.sb("cst", [128, cw])
        K.cstb = Buf("cst")
        fw.dma("sp", K.cst[:], K.consts[:, :], writes=[K.cstb], key="cst")

        def cs(name, rows=128):
            o, n = CONST_OFFS[name]
            return K.cst[0:rows, o:o + n]
        K.cs = cs
        K.identb = fw.sb("identb", [128, 128], BF16); K.onesb = fw.sb("onesb", [128, 128], BF16)
        K.cbb = fw.sb("cbb", [128, 2048], BF16); K.Eb = fw.sb("Eb", [16, 2048], BF16)
        K.cbuf = Buf("constsb")
        fw.op("dve", lambda e: e.tensor_copy(K.identb[:], cs("ident")), reads=[K.cstb], writes=[K.cbuf])
        fw.op("dve", lambda e: e.tensor_copy(K.onesb[:], cs("ones")), reads=[K.cstb], writes=[K.cbuf])
        fw.op("dve", lambda e: e.tensor_copy(K.cbb[:], cs("cb")), reads=[K.cstb], writes=[K.cbuf])
        fw.op("dve", lambda e: e.tensor_copy(K.Eb[:], cs("E", 16)), reads=[K.cstb], writes=[K.cbuf])
        K.bd = fw.sb("bd", [128, 32, 16]); K.bdb = Buf("bd")
        K.epsc = fw.sb("epsc", [128, 1]); K.epsb = Buf("eps")
        fw.op("pool", lambda e: e.memset(K.epsc[:], EPS), writes=[K.epsb])
        if os.environ.get("SKIP_PHASES"):
            fw.op("pool", lambda e: e.memset(K.bd[:], 0.0), writes=[K.bdb])

        phases = [("inproj", phase_inproj), ("gdnprep", phase_gdn_prep), ("gdn", phase_gdn), ("moba", phase_moba),
                  ("merge", phase_merge), ("outproj", phase_outproj), ("ffnup", phase_ffn_up), ("ffndown", phase_ffn_down),
                  ("final", phase_final)]
        import os
        skip = os.environ.get("SKIP_PHASES", "").split(",")
        for name, fn in phases:
            if name in skip:
                continue
            with ExitStack() as pst:
                fn(K, fw, pst)
                fw.barrier()
            if stop == name:
                break
        print(f"[build] instructions={fw.ninst} waits={fw.nwait} cnt={fw.cnt} dsems={len(fw.dsem)}")
    return nc


def evac_engines():
    i = 0
    while True:
        yield "act" if i % 2 == 0 else "dve"
        i += 1


def norm_to_hT(K, fw, st, x_dram, rows, ln_dram, hT, hTb, ps_pair, key):
    lnT = fw.sb(key + "lnT", [128, 16], F32, st); lnb = Buf("lnT")
    fw.dma("sp", lnT[:], ln_dram[0, :].rearrange("(c p) -> p c", p=128), writes=[lnb], key=key + "ln",
           allow_slow_non_contiguous=True)
    xr = Ring(fw, key + "x", 2, [128, D], F32, st)
    xn = Ring(fw, key + "xn", 2, [128, D], BF16, st)
    junk = fw.sb(key + "junk", [128, D], BF16, st); junkb = Buf("junk")
    ss = Ring(fw, key + "ss", 2, [128, 4], F32, st)
    for i, r0 in enumerate(rows):
        xt, xb, xi = xr.next()
        fw.dma("sp", xt[:], x_dram[r0:r0 + 128, :], writes=[xb], key=f"{key}x{xi}")
        sst, ssb, _ = ss.next()
        fw.op("act", lambda e: e.activation(out=junk[:], in_=xt[:], func=AF.Square, accum_out=sst[:, 0:1]),
              reads=[xb], writes=[junkb, ssb])
        fw.op("act", lambda e: e.activation(out=sst[:, 1:2], in_=sst[:, 0:1], func=AF.Ln, scale=1.0 / D, bias=K.epsc[:, 0:1]),
              reads=[ssb, K.epsb], writes=[ssb])
        fw.op("act", lambda e: e.activation(out=sst[:, 2:3], in_=sst[:, 1:2], func=AF.Exp, scale=-0.5), reads=[ssb], writes=[ssb])
        xnt, xnb, _ = xn.next()
        fw.op("dve", lambda e: e.tensor_scalar(xnt[:], xt[:], sst[:, 2:3], None, ALU.mult), reads=[xb, ssb], writes=[xnb])
        pt, ptb = ps_pair[i % 2]
        for kc in range(16):
            fw.op("pe", lambda e: e.transpose(pt[:, kc * 128:(kc + 1) * 128], xnt[:, kc * 128:(kc + 1) * 128], K.identb[:]),
                  reads=[xnb, K.cbuf], writes=[ptb], inc=(kc == 15))
        fw.op("dve", lambda e: e.tensor_tensor(hT[:, :, i * 128:(i + 1) * 128], pt[:].rearrange("p (k t) -> p k t", k=16),
                                              lnT[:, :].unsqueeze(2).to_broadcast([128, 16, 128]), ALU.mult),
              reads=[ptb, lnb], writes=[hTb[i]])


def load_wslab(K, fw, wring, W_dram, c0, ncols, nk, key):
    wt, wb, wi = wring.next()
    src = W_dram[:, c0:c0 + ncols].rearrange("(kc p) c -> p kc c", p=128)
    for k0 in range(0, nk, 4):
        k1 = min(nk, k0 + 4)
        fw.dma("pool", wt[:, k0:k1, 0:ncols], src[:, k0:k1, :], writes=[wb] if k0 == 0 else [], key=f"{key}{wi}")
    wb.w = (f"{key}{wi}", fw.dcnt[f"{key}{wi}"])
    return wt, wb


def gemm_fm(K, fw, wt, wb, ncols, nk, act, act_bufs, tiles, psring, evac):
    nj = (ncols + 127) // 128
    for j in range(nj):
        m = min(128, ncols - j * 128)
        for (t0, tn) in tiles:
            pt, pb = psring.next()
            for kc in range(nk):
                fw.op("pe", lambda e: e.matmul(pt[0:m, 0:tn], wt[:, kc, j * 128:j * 128 + m], act[:, kc, t0:t0 + tn],
                                               start=(kc == 0), stop=(kc == nk - 1)),
                      reads=[wb] + act_bufs(t0, tn), writes=[pb], inc=(kc == nk - 1))
            evac(pt, pb, j, m, t0, tn)


def gemm_tm(K, fw, wt, wb, ncols, nk, act, act_bufs, subs, psring, evac):
    for s in subs:
        pt, pb = psring.next()
        for kc in range(nk):
            fw.op("pe", lambda e: e.matmul(pt[:, 0:ncols], act[:, kc, s * 128:(s + 1) * 128], wt[:, kc, 0:ncols],
                                           start=(kc == 0), stop=(kc == nk - 1)),
                  reads=[wb] + act_bufs(s * 128, 128), writes=[pb], inc=(kc == nk - 1))
        evac(pt, pb, s)


class PsRing:
    def __init__(self, fw, st, n, shape=(128, 512), dt=F32, name="ps"):
        self.t = [fw.ps(f"{name}{i}", shape, dt, st) for i in range(n)]
        self.b = [Buf(f"{name}{i}") for i in range(n)]
        self.i = 0

    def next(self):
        i = self.i
        self.i = (i + 1) % len(self.t)
        return self.t[i], self.b[i]


def phase_inproj(K, fw, st):
    hT = fw.sb("hT", [128, 16, NQ], BF16, st)
    hTb = [Buf(f"hT{i}") for i in range(17)]
    wring = Ring(fw, "wsl", 2, [128, 16, 512], BF16, st)
    stg = Ring(fw, "stg", 4, [128, 512], F32, st)
    stgb = Ring(fw, "stgb", 2, [128, 512], BF16, st)
    ev = evac_engines()

    def act_bufs(t0, tn):
        return hTb[t0 // 128:(t0 + tn + 127) // 128]

    def store_fm(dst, dbuf, row0, tok0, func=None):
        def evac(pt, pb, j, m, t0, tn):
            s, sb_, si = stg.next()
            if func is None:
                ek = next(ev)
                if ek == "act":
                    fw.op("act", lambda e: e.activation(out=s[0:m, 0:tn], in_=pt[0:m, 0:tn], func=AF.Copy), reads=[pb], writes=[sb_])
                else:
                    fw.op("dve", lambda e: e.tensor_copy(s[0:m, 0:tn], pt[0:m, 0:tn]), reads=[pb], writes=[sb_])
            else:
                fw.op("act", lambda e: e.activation(out=s[0:m, 0:tn], in_=pt[0:m, 0:tn], func=func), reads=[pb], writes=[sb_])
            fw.dma("sp", dst[row0 + j * 128:row0 + j * 128 + m, tok0 + t0:tok0 + t0 + tn], s[0:m, 0:tn], reads=[sb_], writes=[],
                   key=f"st{si}")
        return evac

    for pas in range(2):
        with ExitStack() as pst:
            if pas == 0:
                rows = [i * 128 for i in range(15)]; tiles = CTXTILES; tok0 = 0; nsub = 15
            else:
                rows = [Q0 + i * 128 for i in range(17)]; tiles = QTILES; tok0 = Q0; nsub = 17
            pp = [(fw.ps(f"ptr{i}", [128, 2048], BF16, pst), Buf(f"ptr{i}")) for i in range(2)]
            norm_to_hT(K, fw, pst, K.x, rows, K.ln1, hT, hTb, pp, f"n1{pas}")
            fw.barrier()
        import os
        BIS = int(os.environ.get("BISECT", "99"))
        if BIS == 1:
            return
        with ExitStack() as pst:
            psr = PsRing(fw, pst, 8)
            jobs = []
            if pas == 1:
                jobs.append(("fm", C_QKVA, 1024, K.qkvA, 0, None))
            jobs.append(("fm", C_QKVA + 1024, 2048, K.qkvA, 1024, None))
            jobs.append(("bd", C_BETA, 16, None, 0, None))
            if pas == 1:
                jobs.append(("ztm", C_Z, 1024, K.zTM, 0, None))
                jobs.append(("fm", C_QKVB, 1024, K.qkB, 0, None))
            jobs.append(("fm", C_QKVB + 1024, 1024, K.qkB, 1024, None))
            jobs.append(("vtm", C_QKVB + 2048, 1024, K.vB, 0, None))
            if pas == 1:
                jobs.append(("fm", C_GA, 2048, K.gA, 0, AF.Sigmoid))
                jobs.append(("fm", C_GB, 2048, K.gB, 0, AF.Sigmoid))
            if BIS < 10:
                jobs = jobs[:BIS - 1]
            if os.environ.get("INPROJ_JOBS"):
                jobs = [j for j in jobs if j[0] in os.environ["INPROJ_JOBS"].split(",")]
            for (mode, wc0, ncols, dst, drow0, func) in jobs:
                for c0 in range(0, ncols, 512):
                    ncs = min(512, ncols - c0)
                    wt, wb = load_wslab(K, fw, wring, K.w_in, wc0 + c0, ncs, 16, "wsl")
                    if mode == "fm":
                        dtok0 = tok0 if dst in (K.qkvA, K.qkB) else 0
                        gemm_fm(K, fw, wt, wb, ncs, 16, hT, act_bufs, tiles, psr,
                                store_fm(dst, K.B, drow0 + c0, dtok0, func))
                    elif mode == "bd":
                        def evac_bd(pt, pb, s):
                            fw.op("dve", lambda e: e.tensor_copy(K.bd[:, tok0 // 128 + s, :], pt[:, 0:16]), reads=[pb], writes=[K.bdb])
                        gemm_tm(K, fw, wt, wb, 16, 16, hT, act_bufs, range(nsub), psr, evac_bd)
                    elif mode == "ztm":
                        def evac_z(pt, pb, s, c0=c0):
                            sg, sb_, si = stg.next()
                            fw.op("act", lambda e: e.activation(out=sg[:, :], in_=pt[:, :], func=AF.Copy), reads=[pb], writes=[sb_])
                            fw.dma("sp", K.zTM[s * 128:(s + 1) * 128, c0:c0 + 512], sg[:, :], reads=[sb_], key=f"st{si}")
                        gemm_tm(K, fw, wt, wb, 512, 16, hT, act_bufs, range(nsub), psr, evac_z)
                    elif mode == "vtm":
                        def evac_v(pt, pb, s, c0=c0):
                            sg, sb_, si = stgb.next()
                            fw.op("dve", lambda e: e.tensor_copy(sg[:, :], pt[:, :]), reads=[pb], writes=[sb_])
                            fw.dma("sp", K.vB[tok0 + s * 128:tok0 + (s + 1) * 128, c0:c0 + 512], sg[:, :], reads=[sb_], key=f"stb{si}")
                        gemm_tm(K, fw, wt, wb, 512, 16, hT, act_bufs, range(nsub), psr, evac_v)
            fw.barrier()


def load_rows_T(K, fw, st, src_dram, nrows, ncols, name):
    nch = ncols // 128
    out = fw.sb(name, [128, nch, nrows], F32, st); ob = Buf(name)
    PC = 16
    with ExitStack() as pst:
        raw = fw.sb(name + "raw", [nrows, PC * 128], F32, pst); rb = Buf(name + "raw")
        pt = fw.ps(name + "ps", [128, 512], F32, pst); pb = Buf(name + "ps")
        for c0 in range(0, nch, PC):
            c1 = min(nch, c0 + PC)
            fw.dma("sp", raw[:, 0:(c1 - c0) * 128], src_dram[:, c0 * 128:c1 * 128], writes=[rb], key=name)
            for c in range(c0, c1):
                fw.op("pe", lambda e: e.transpose(pt[:, (c - c0) * nrows:(c - c0 + 1) * nrows], raw[0:nrows, (c - c0) * 128:(c - c0 + 1) * 128],
                                                  K.cs("ident")[0:nrows, 0:nrows]),
                      reads=[rb, K.cstb], writes=[pb], inc=(c == c1 - 1))
            fw.op("dve", lambda e: e.tensor_copy(out[:, c0:c1, :], pt[:, 0:(c1 - c0) * nrows].rearrange("p (c j) -> p c j", j=nrows)),
                  reads=[pb], writes=[ob])
        fw.barrier()
    return out, ob


def phase_gdn_prep(K, fw, st):
    cw, cwb = load_rows_T(K, fw, st, K.gdn_conv, 4, 3072, "gcw")
    HN = 2048
    xin = Ring(fw, "gxin", 2, [128, 3 + HN], F32, st)
    t1r = Ring(fw, "gt1", 2, [128, HN], F32, st)
    yr = Ring(fw, "gy", 2, [128, HN], F32, st)
    sqr = Ring(fw, "gsq", 2, [128, HN], F32, st)
    rir = Ring(fw, "gri", 2, [128, HN], F32, st)
    ob = Ring(fw, "gob", 2, [128, HN], BF16, st)
    psr = PsRing(fw, st, 8)
    epsl = K.epsc
    for h in range(8):
        for typ in range(3):
            ntot = NQ if typ == 0 else L
            tok0 = Q0 if typ == 0 else 0
            row0 = typ * 1024 + h * 128
            ch = typ * 8 + h
            for hs in range(0, ntot, HN):
                n = min(HN, ntot - hs)
                xt, xb, xi = xin.next()
                if hs == 0:
                    fw.op("pool", lambda e: e.memset(xt[:, 0:3], 0.0), writes=[xb])
                    fw.dma("sp", xt[:, 3:3 + n], K.qkvA[row0:row0 + 128, tok0:tok0 + n], reads=[xb], writes=[xb], key=f"gx{xi}")
                else:
                    fw.dma("sp", xt[:, 0:3 + n], K.qkvA[row0:row0 + 128, tok0 + hs - 3:tok0 + hs + n], writes=[xb], key=f"gx{xi}")
                fw.flush()
                t1, t1b, _ = t1r.next(); y, yb, _ = yr.next()
                fw.op("dve", lambda e: e.tensor_scalar(t1[:, 0:n], xt[:, 0:n], cw[:, ch, 0:1], None, ALU.mult), reads=[xb, cwb], writes=[t1b])
                for j in (1, 2, 3):
                    fw.op("dve", lambda e: e.scalar_tensor_tensor(t1[:, 0:n], xt[:, j:j + n], cw[:, ch, j:j + 1], t1[:, 0:n], ALU.mult, ALU.add),
                          reads=[xb, cwb, t1b], writes=[t1b])
                fw.op("act", lambda e: e.activation(out=y[:, 0:n], in_=t1[:, 0:n], func=AF.Silu), reads=[t1b], writes=[yb])
                ot, otb, oi = ob.next()
                if typ == 2:
                    fw.op("act", lambda e: e.activation(out=ot[:, 0:n], in_=y[:, 0:n], func=AF.Copy), reads=[yb], writes=[otb])
                    dst = K.vsT
                else:
                    sq, sqb, _ = sqr.next(); rinv, rinvb, _ = rir.next()
                    fw.op("act", lambda e: e.activation(out=sq[:, 0:n], in_=y[:, 0:n], func=AF.Square), reads=[yb], writes=[sqb])
                    for t0 in range(0, n, 512):
                        tn = min(512, n - t0)
                        pt, pb = psr.next()
                        fw.op("pe", lambda e: e.matmul(pt[:, 0:tn], K.cs("ones"), sq[:, t0:t0 + tn], start=True, stop=True),
                              reads=[sqb, K.cstb], writes=[pb])
                        fw.op("act", lambda e: e.activation(out=rinv[:, t0:t0 + tn], in_=pt[:, 0:tn], func=AF.Ln, bias=epsl[:, 0:1]),
                              reads=[pb, K.epsb], writes=[rinvb])
                    fw.op("act", lambda e: e.activation(out=rinv[:, 0:n], in_=rinv[:, 0:n], func=AF.Exp, scale=-0.5), reads=[rinvb], writes=[rinvb])
                    sc = (128.0 ** -0.5) if typ == 0 else 1.0
                    fw.op("dve", lambda e: e.scalar_tensor_tensor(ot[:, 0:n], y[:, 0:n], sc, rinv[:, 0:n], ALU.mult, ALU.mult),
                          reads=[yb, rinvb], writes=[otb])
                    dst = K.qnT if typ == 0 else K.knT
                fw.dma("sp", dst[h * 128:(h + 1) * 128, hs:hs + n], ot[:, 0:n], reads=[otb], key=f"go{oi}", defer=True)


def phase_gdn(K, fw, st):
    P = 128
    NCH = L // 128
    QCH0 = Q0 // 128

    def T(name, shape, dt=F32):
        return fw.sb(name, shape, dt, st)

    def bc_h(ap2):
        return ap2.unsqueeze(1).to_broadcast([P, 8, P])

    def bc_s(ap8):
        return ap8.unsqueeze(2).to_broadcast([P, 8, P])

    onec = T("g_onec", [P, 1]); oneb = Buf("onec")
    fw.op("pool", lambda e: e.memset(onec[:], 1.0), writes=[oneb])
    hp = T("g_hp", [P, 24]); hpb = Buf("hp")
    fw.dma("sp", hp[:, 0:8], K.a_log[0:1, :].partition_broadcast(P), writes=[hpb], key="ghp")
    fw.dma("sp", hp[:, 8:16], K.dt_bias[0:1, :].partition_broadcast(P), writes=[hpb], key="ghp")
    fw.op("act", lambda e: e.activation(out=hp[:, 16:24], in_=hp[:, 0:8], func=AF.Exp), reads=[hpb], writes=[hpb])
    fw.op("dve", lambda e: e.tensor_scalar(hp[:, 16:24], hp[:, 16:24], -1.0, None, ALU.mult), reads=[hpb], writes=[hpb])
    gnw = T("g_gnw", [P, P]); gnwb = Buf("gnw")
    fw.dma("sp", gnw[:], K.gdn_norm[0:1, :].partition_broadcast(P), writes=[gnwb], key="ggn")
    S = T("g_S", [P, 8, P]); Sb = T("g_Sb", [P, 8, P], BF16)
    Sbuf = [Buf(f"S{h}") for h in range(8)]; Sbbuf = [Buf(f"Sb{h}") for h in range(8)]
    fw.op("pool", lambda e: e.memset(S[:], 0.0), writes=Sbuf)
    fw.op("pool", lambda e: e.memset(Sb[:], 0.0), writes=Sbbuf)
    big = PsRing(fw, st, 2, (P, 1024), F32, "gbig")
    seq = PsRing(fw, st, 3, (P, 512), F32, "gseq")
    ptr = fw.ps("gtr", [P, 1024], BF16, st); ptrb = Buf("gtr")

    class CB:
        pass
    cbs = []
    for r in range(2):
        c = CB()
        def mk(name, shape, dt=F32, c=c, r=r):
            setattr(c, name, T(f"g{r}_{name}", shape, dt))
            setattr(c, name + "_b", Buf(f"g{r}_{name}"))
        for nm in ["knc", "vsc", "qnc"]:
            mk(nm, [P, 8, P], BF16)
        mk("zc", [P, 1024]); mk("sz", [P, 1024])
        mk("sm", [P, 96])
        mk("dg", [P, 8, P]); mk("RB", [P, 8, P]); mk("ERB", [P, 8, P], BF16); mk("tmp", [P, 8, P])
        mk("tS", [P, 8, P]); mk("tU", [P, 8, P])
        for nm in ["KgT", "QgT", "N", "NT", "AT", "Kd", "X0", "X1", "XT0", "XT1", "PT0", "PT1", "og"]:
            mk(nm, [P, 8, P], BF16)
        mk("Vb", [P, 8, P]); mk("oall", [P, 8, P]); mk("osq", [P, 8, P])
        c.R = T(f"g{r}_R", [P, 8, P], BF16); c.R_b = [Buf(f"R{h}") for h in range(8)]
        c.vn = T(f"g{r}_vn", [P, 8, P], BF16); c.vn_b = [Buf(f"vn{h}") for h in range(8)]
        c.oT = T(f"g{r}_oT", [P, 8, P], BF16); c.oT_b = Buf("oT")
        cbs.append(c)

    ident = K.cs("ident"); ones = K.cs("ones"); tri = K.cs("tri"); mbS = K.cs("mbS"); mbU = K.cs("mbU")
    CS = [K.cstb]

    import os
    chunks = range(NCH)
    if os.environ.get("GDN_CHUNKS"):
        a, b = os.environ["GDN_CHUNKS"].split(":")
        chunks = range(int(a), int(b))
    GSEC = os.environ.get("GDN_SEC", "Z")
    pending = None
    for n in chunks:
        c = cbs[n % 2]
        isq = n >= QCH0
        t0 = n * P
        sm = c.sm; smB = c.sm_b
        fw.dma("sp", c.knc[:], K.knT[:, t0:t0 + P].rearrange("(h d) t -> d h t", d=P), writes=[c.knc_b], key=f"gk{n % 2}")
        fw.dma("sp", c.vsc[:], K.vsT[:, t0:t0 + P].rearrange("(h d) t -> d h t", d=P), writes=[c.vsc_b], key=f"gv{n % 2}")
        if isq:
            q0 = t0 - Q0
            fw.dma("sp", c.qnc[:], K.qnT[:, q0:q0 + P].rearrange("(h d) t -> d h t", d=P), writes=[c.qnc_b], key=f"gq{n % 2}")
            fw.dma("sp", c.zc[:], K.zTM[q0:q0 + P, :], writes=[c.zc_b], key=f"gz{n % 2}")
        fw.flush()
        fw.op("dve", lambda e: e.tensor_tensor(sm[:, 0:8], K.bd[:, n, 8:16], hp[:, 8:16], ALU.add), reads=[K.bdb, hpb], writes=[smB])
        fw.op("act", lambda e: e.activation(out=sm[:, 8:16], in_=sm[:, 0:8], func=AF.Exp), reads=[smB], writes=[smB])
        fw.op("act", lambda e: e.activation(out=sm[:, 8:16], in_=sm[:, 8:16], func=AF.Ln, bias=onec[:, 0:1]), reads=[smB, oneb], writes=[smB])
        fw.op("dve", lambda e: e.tensor_tensor(sm[:, 16:24], sm[:, 8:16], hp[:, 16:24], ALU.mult), reads=[smB, hpb], writes=[smB])
        ps, psb = seq.next()
        fw.op("pe", lambda e: e.matmul(ps[:, 0:8], tri, sm[:, 16:24], start=True, stop=True), reads=CS + [smB], writes=[psb])
        fw.op("dve", lambda e: e.tensor_copy(sm[:, 24:32], ps[:, 0:8]), reads=[psb], writes=[smB])
        fw.op("act", lambda e: e.activation(out=sm[:, 32:40], in_=K.bd[:, n, 0:8], func=AF.Exp, scale=-1.0), reads=[K.bdb], writes=[smB])
        fw.op("dve", lambda e: e.tensor_scalar(sm[:, 32:40], sm[:, 32:40], 1.0, None, ALU.add), reads=[smB], writes=[smB])
        fw.op("dve", lambda e: e.reciprocal(sm[:, 32:40], sm[:, 32:40]), reads=[smB], writes=[smB])
        fw.op("dve", lambda e: e.tensor_scalar(sm[:, 40:48], sm[:, 32:40], -1.0, None, ALU.mult), reads=[smB], writes=[smB])
        gc = sm[:, 24:32]; beta = sm[:, 32:40]; negb = sm[:, 40:48]
        if GSEC <= "B":
            continue
        fw.op("dve", lambda e: e.tensor_tensor(c.dg[:], bc_h(ident), bc_s(gc), ALU.mult), reads=CS + [smB], writes=[c.dg_b])
        pb_, pbb = big.next()
        for j in range(2):
            fw.op("pe", lambda e: e.matmul(pb_[:, j * 512:(j + 1) * 512], ones,
                                           c.dg[:].rearrange("p h t -> p (h t)")[:, j * 512:(j + 1) * 512], start=True, stop=True),
                  reads=CS + [c.dg_b], writes=[pbb], inc=(j == 1))
        for j in range(2):
            fw.op("act", lambda e: e.activation(out=c.RB[:].rearrange("p h t -> p (h t)")[:, j * 512:(j + 1) * 512],
                                                in_=pb_[:, j * 512:(j + 1) * 512], func=AF.Copy), reads=[pbb], writes=[c.RB_b])
        fw.op("act", lambda e: e.activation(out=c.ERB[:], in_=c.RB[:], func=AF.Exp), reads=[c.RB_b], writes=[c.ERB_b])
        fw.op("act", lambda e: e.activation(out=sm[:, 48:56], in_=c.RB[:, :, P - 1], func=AF.Exp), reads=[c.RB_b], writes=[smB])
        fw.op("dve", lambda e: e.tensor_tensor(sm[:, 64:72], c.RB[:, :, P - 1], gc, ALU.subtract), reads=[c.RB_b, smB], writes=[smB])
        fw.op("act", lambda e: e.activation(out=sm[:, 56:64], in_=sm[:, 64:72], func=AF.Exp), reads=[smB], writes=[smB])
        egl = sm[:, 48:56]; eglc = sm[:, 56:64]
        fw.op("dve", lambda e: e.tensor_tensor(c.tmp[:], c.RB[:], bc_s(gc), ALU.subtract), reads=[c.RB_b, smB], writes=[c.tmp_b])
        fw.op("dve", lambda e: e.tensor_tensor(c.tS[:], c.tmp[:], bc_h(mbS), ALU.add), reads=CS + [c.tmp_b], writes=[c.tS_b])
        fw.op("act", lambda e: e.activation(out=c.tS[:], in_=c.tS[:], func=AF.Exp, scale=-1.0), reads=[c.tS_b], writes=[c.tS_b])
        fw.op("dve", lambda e: e.tensor_tensor(c.tS[:], c.tS[:], bc_s(negb), ALU.mult), reads=[c.tS_b, smB], writes=[c.tS_b])
        if isq:
            fw.op("dve", lambda e: e.tensor_tensor(c.tU[:], c.tmp[:], bc_h(mbU), ALU.add), reads=CS + [c.tmp_b], writes=[c.tU_b])
            fw.op("act", lambda e: e.activation(out=c.tU[:], in_=c.tU[:], func=AF.Exp), reads=[c.tU_b], writes=[c.tU_b])
        if GSEC <= "C":
            continue
        fw.op("dve", lambda e: e.tensor_tensor(c.KgT[:], c.knc[:], c.ERB[:], ALU.mult), reads=[c.knc_b, c.ERB_b], writes=[c.KgT_b])
        if isq:
            fw.op("dve", lambda e: e.tensor_tensor(c.QgT[:], c.qnc[:], c.ERB[:], ALU.mult), reads=[c.qnc_b, c.ERB_b], writes=[c.QgT_b])
        if GSEC <= "D":
            continue
        pb_, pbb = big.next()
        for h in range(8):
            fw.op("pe", lambda e: e.matmul(pb_[:, h * P:(h + 1) * P], c.knc[:, h, :], c.knc[:, h, :], start=True, stop=True),
                  reads=[c.knc_b], writes=[pbb], inc=(h == 7))
        fw.op("dve", lambda e: e.tensor_tensor(c.N[:], pb_[:].rearrange("p (h t) -> p h t", h=8), c.tS[:], ALU.mult),
              reads=[pbb, c.tS_b], writes=[c.N_b])
        if isq:
            pb_, pbb = big.next()
            for h in range(8):
                fw.op("pe", lambda e: e.matmul(pb_[:, h * P:(h + 1) * P], c.knc[:, h, :], c.qnc[:, h, :], start=True, stop=True),
                      reads=[c.knc_b, c.qnc_b], writes=[pbb], inc=(h == 7))
            fw.op("dve", lambda e: e.tensor_tensor(c.AT[:], pb_[:].rearrange("p (h t) -> p h t", h=8), c.tU[:], ALU.mult),
                  reads=[pbb, c.tU_b], writes=[c.AT_b])
        if GSEC <= "E":
            continue
        ptr3 = ptr[:].rearrange("p (h t) -> p h t", h=8)
        for h in range(8):
            fw.op("pe", lambda e: e.transpose(ptr[:, h * P:(h + 1) * P], c.knc[:, h, :], K.identb[:]), reads=[c.knc_b, K.cbuf], writes=[ptrb], inc=(h == 7))
        fw.op("dve", lambda e: e.tensor_tensor(c.Kd[:], ptr3, bc_s(eglc), ALU.mult), reads=[ptrb, smB], writes=[c.Kd_b])
        for h in range(8):
            fw.op("pe", lambda e: e.transpose(ptr[:, h * P:(h + 1) * P], c.vsc[:, h, :], K.identb[:]), reads=[c.vsc_b, K.cbuf], writes=[ptrb], inc=(h == 7))
        fw.op("act", lambda e: e.activation(out=c.Vb[:], in_=ptr3, func=AF.Copy), reads=[ptrb], writes=[c.Vb_b])
        if GSEC <= "F":
            continue
        for h in range(8):
            fw.op("pe", lambda e: e.transpose(ptr[:, h * P:(h + 1) * P], c.N[:, h, :], K.identb[:]), reads=[c.N_b, K.cbuf], writes=[ptrb], inc=(h == 7))
        GD = os.environ.get("GDN_DBG", "")
        if "1" not in GD:
            fw.op("dve", lambda e: e.tensor_copy(c.NT[:], ptr3), reads=[ptrb], writes=[c.NT_b])
        if "2" not in GD:
            fw.op("dve", lambda e: e.tensor_tensor(c.PT0[:], ptr3, bc_h(ident), ALU.add), reads=[ptrb] + CS, writes=[c.PT0_b])
        X, Xb, XT, XTb = c.N, c.N_b, c.NT, c.NT_b
        PT, PTb = c.PT0, c.PT0_b
        for lvl in range(1, 7):
            if lvl > int(os.environ.get("GDN_LVL", "6")):
                break
            Xn, Xnb = (c.X0, c.X0_b) if lvl % 2 else (c.X1, c.X1_b)
            XTn, XTnb = (c.XT0, c.XT0_b) if lvl % 2 else (c.XT1, c.XT1_b)
            PTn, PTnb = (c.PT1, c.PT1_b) if lvl % 2 else (c.PT0, c.PT0_b)
            pa, pab = big.next()
            for h in range(8):
                fw.op("pe", lambda e: e.matmul(pa[:, h * P:(h + 1) * P], XT[:, h, :], X[:, h, :], start=True, stop=True),
                      reads=[Xb, XTb], writes=[pab], inc=(h == 7))
            for hh in range(2):
                fw.op("act", lambda e: e.activation(out=Xn[:, 4 * hh:4 * hh + 4, :], in_=pa[:, hh * 512:(hh + 1) * 512].rearrange("p (h t) -> p h t", h=4),
                                                    func=AF.Copy), reads=[pab], writes=[Xnb])
            if lvl < 6:
                pb2, pb2b = big.next()
                for h in range(8):
                    fw.op("pe", lambda e: e.matmul(pb2[:, h * P:(h + 1) * P], X[:, h, :], XT[:, h, :], start=True, stop=True),
                          reads=[Xb, XTb], writes=[pb2b], inc=(h == 7))
                for hh in range(2):
                    fw.op("act", lambda e: e.activation(out=XTn[:, 4 * hh:4 * hh + 4, :], in_=pb2[:, hh * 512:(hh + 1) * 512].rearrange("p (h t) -> p h t", h=4),
                                                        func=AF.Copy), reads=[pb2b], writes=[XTnb])
            pc, pcb = big.next()
            for h in range(8):
                fw.op("pe", lambda e: e.matmul(pc[:, h * P:(h + 1) * P], Xn[:, h, :], PT[:, h, :], start=True, stop=True),
                      reads=[Xnb, PTb], writes=[pcb], inc=(h == 7))
            fw.op("dve", lambda e: e.tensor_tensor(PTn[:], pc[:].rearrange("p (h t) -> p h t", h=8), PT[:], ALU.add), reads=[pcb, PTb], writes=[PTnb])
            X, Xb, XT, XTb, PT, PTb = Xn, Xnb, XTn, XTnb, PTn, PTnb
            if pending is not None:
                next(pending, None)
        if GSEC <= "G":
            continue
        TT, TTb = c.X0, c.X0_b
        fw.op("dve", lambda e: e.tensor_tensor(TT[:], PT[:], bc_s(beta), ALU.mult), reads=[PTb, smB], writes=[TTb])
        if pending is not None:
            for _ in pending:
                pass

        def seq_gen(c=c, n=n, isq=isq, q0=(t0 - Q0), TT=TT, TTb=TTb, sm=sm, smB=smB, egl=egl):
            for g in range(2):
                hs = range(4 * g, 4 * g + 4)
                Sg = [Sbuf[h] for h in hs]; Sbg = [Sbbuf[h] for h in hs]
                gsl = slice(4 * g, 4 * g + 4)
                p1, p1b = seq.next()
                for h in hs:
                    fw.op("pe", lambda e: e.matmul(p1[:, (h % 4) * P:(h % 4 + 1) * P], c.KgT[:, h, :], Sb[:, h, :], start=True, stop=True),
                          reads=[c.KgT_b] + Sbg, writes=[p1b], inc=(h % 4 == 3))
                fw.op("dve", lambda e: e.tensor_tensor(c.R[:, gsl, :], c.Vb[:, gsl, :], p1[:].rearrange("p (h t) -> p h t", h=4), ALU.subtract),
                      reads=[p1b, c.Vb_b], writes=[c.R_b[g]])
                p2, p2b = seq.next()
                for h in hs:
                    fw.op("pe", lambda e: e.matmul(p2[:, (h % 4) * P:(h % 4 + 1) * P], TT[:, h, :], c.R[:, h, :], start=True, stop=True),
                          reads=[TTb, c.R_b[g]], writes=[p2b], inc=(h % 4 == 3))
                fw.op("act", lambda e: e.activation(out=c.vn[:, gsl, :], in_=p2[:].rearrange("p (h t) -> p h t", h=4), func=AF.Copy),
                      reads=[p2b], writes=[c.vn_b[g]])
                yield
                if isq:
                    po, pob = seq.next()
                    for h in hs:
                        sl = slice((h % 4) * P, (h % 4 + 1) * P)
                        fw.op("pe", lambda e: e.matmul(po[:, sl], c.QgT[:, h, :], Sb[:, h, :], start=True, stop=False),
                              reads=[c.QgT_b] + Sbg, writes=[pob], inc=False)
                        fw.op("pe", lambda e: e.matmul(po[:, sl], c.AT[:, h, :], c.vn[:, h, :], start=False, stop=True),
                              reads=[c.AT_b, c.vn_b[g]], writes=[pob], inc=(h % 4 == 3))
                    fw.op("act", lambda e: e.activation(out=c.oall[:, gsl, :], in_=po[:].rearrange("p (h t) -> p h t", h=4), func=AF.Copy),
                          reads=[pob], writes=[c.oall_b])
                    yield
                p3, p3b = seq.next()
                for h in hs:
                    fw.op("pe", lambda e: e.matmul(p3[:, (h % 4) * P:(h % 4 + 1) * P], c.Kd[:, h, :], c.vn[:, h, :], start=True, stop=True),
                          reads=[c.Kd_b, c.vn_b[g]], writes=[p3b], inc=(h % 4 == 3))
                fw.op("dve", lambda e: e.tensor_tensor(S[:, gsl, :], S[:, gsl, :], bc_s(egl)[:, gsl, :], ALU.mult), reads=Sg + [smB], writes=Sg)
                fw.op("dve", lambda e: e.tensor_tensor(S[:, gsl, :], S[:, gsl, :], p3[:].rearrange("p (h t) -> p h t", h=4), ALU.add),
                      reads=Sg + [p3b], writes=Sg)
                fw.op("act", lambda e: e.activation(out=Sb[:, gsl, :], in_=S[:, gsl, :], func=AF.Copy), reads=Sg, writes=Sbg)
                yield
            if isq:
                fw.op("act", lambda e: e.activation(out=c.sz[:], in_=c.zc[:], func=AF.Silu), reads=[c.zc_b], writes=[c.sz_b])
                fw.op("act", lambda e: e.activation(out=c.osq[:], in_=c.oall[:], func=AF.Square), reads=[c.oall_b], writes=[c.osq_b])
                fw.op("dve", lambda e: e.tensor_reduce(sm[:, 72:80], c.osq[:], AX.X, ALU.add), reads=[c.osq_b], writes=[smB])
                fw.op("act", lambda e: e.activation(out=sm[:, 80:88], in_=sm[:, 72:80], func=AF.Ln, scale=1.0 / 128, bias=K.epsc[:, 0:1]), reads=[smB, K.epsb], writes=[smB])
                fw.op("act", lambda e: e.activation(out=sm[:, 80:88], in_=sm[:, 80:88], func=AF.Exp, scale=-0.5), reads=[smB], writes=[smB])
                fw.op("dve", lambda e: e.tensor_tensor(c.osq[:], c.oall[:], bc_s(sm[:, 80:88]), ALU.mult), reads=[c.oall_b, smB], writes=[c.osq_b])
                fw.op("dve", lambda e: e.tensor_tensor(c.osq[:], c.osq[:], bc_h(gnw[:, :]), ALU.mult), reads=[c.osq_b, gnwb], writes=[c.osq_b])
                fw.op("dve", lambda e: e.tensor_tensor(c.og[:], c.osq[:], c.sz[:].rearrange("p (h t) -> p h t", h=8), ALU.mult), reads=[c.osq_b, c.sz_b], writes=[c.og_b])
                for h in range(8):
                    fw.op("pe", lambda e: e.transpose(ptr[:, h * P:(h + 1) * P], c.og[:, h, :], K.identb[:]), reads=[c.og_b, K.cbuf], writes=[ptrb], inc=(h == 7))
                fw.op("dve", lambda e: e.tensor_copy(c.oT[:], ptr3), reads=[ptrb], writes=[c.oT_b])
                fw.dma("sp", K.oaT[:, q0:q0 + P].rearrange("(h d) t -> d h t", d=P), c.oT[:], reads=[c.oT_b], key=f"go{n % 2}", defer=True)
            yield

        pending = seq_gen()
    if pending is not None:
        for _ in pending:
            pass


def phase_moba(K, fw, st):
    P = 128
    PI = float(np.pi)

    def T(name, shape, dt=F32):
        return fw.sb(name, shape, dt, st)
    CS = [K.cstb]
    posi = T("m_posi", [32, L], I32); posb = Buf("posi")
    fw.dma("sp", posi[:], K.pos[0:1, :].partition_broadcast(32), writes=[posb], key="mpos")
    ang = T("m_ang", [32, L]); angb = Buf("ang")
    fw.op("dve", lambda e: e.tensor_copy(ang[:], posi[:]), reads=[posb], writes=[angb])
    fw.op("dve", lambda e: e.tensor_scalar(ang[:], ang[:], K.cs("invf")[0:32, 0:1], None, ALU.mult), reads=[angb] + CS, writes=[angb])
    tabs = []
    wk = T("m_wk", [32, L]); wkb = Buf("wk")
    wk2 = T("m_wk2", [32, L]); wk2b = Buf("wk2")
    for name, shift in (("sin", 0.0), ("cos", PI / 2)):
        tab = T("m_" + name, [32, L]); tb = Buf(name)
        fw.op("dve", lambda e: e.tensor_scalar(wk[:], ang[:], 1.0 / (2 * PI), shift / (2 * PI) + 0.5, ALU.mult, ALU.add), reads=[angb], writes=[wkb])
        fw.op("dve", lambda e: e.tensor_copy(posi[:], wk[:]), reads=[wkb, posb], writes=[posb])
        fw.op("dve", lambda e: e.tensor_copy(wk[:], posi[:]), reads=[posb], writes=[wkb])
        fw.op("dve", lambda e: e.tensor_scalar(wk2[:], ang[:], shift, None, ALU.add), reads=[angb], writes=[wk2b])
        fw.op("dve", lambda e: e.scalar_tensor_tensor(tab[:], wk[:], -2 * PI, wk2[:], ALU.mult, ALU.add), reads=[wkb, wk2b], writes=[tb])
        fw.op("dve", lambda e: e.tensor_scalar(wk[:], tab[:], -PI, 2 * PI, ALU.is_lt, ALU.mult), reads=[tb], writes=[wkb])
        fw.op("dve", lambda e: e.tensor_tensor(tab[:], tab[:], wk[:], ALU.add), reads=[tb, wkb], writes=[tb])
        fw.op("dve", lambda e: e.tensor_scalar(wk[:], tab[:], PI, -2 * PI, ALU.is_gt, ALU.mult), reads=[tb], writes=[wkb])
        fw.op("dve", lambda e: e.tensor_tensor(tab[:], tab[:], wk[:], ALU.add), reads=[tb, wkb], writes=[tb])
        fw.op("dve", lambda e: e.tensor_scalar(tab[:], tab[:], -PI, PI, ALU.max, ALU.min), reads=[tb], writes=[tb])
        fw.op("act", lambda e: e.activation(out=tab[:], in_=tab[:], func=AF.Sin), reads=[tb], writes=[tb])
        tabs.append((tab, tb))
    (sinT, sinb), (cosT, cosb) = tabs
    blk = [7] + [8 + (s - 1) // 2 for s in range(1, 17)]
    cv = T("m_cv", [P, 16]); cvb = Buf("cv")
    fw.dma("sp", cv[:], K.cvalid[0:1, :].partition_broadcast(P), writes=[cvb], key="mcv")
    vb1 = T("m_vb1", [P, 17, 16]); vb2 = T("m_vb2", [P, 17, 16]); nown = T("m_nown", [P, 17, 16]); vbb = Buf("vb")
    fw.op("pool", lambda e: e.memset(vb1[:], 0.0), writes=[vbb])
    fw.op("pool", lambda e: e.memset(vb2[:], 0.0), writes=[vbb])
    fw.op("pool", lambda e: e.memset(nown[:], 1.0), writes=[vbb])
    for s_ in range(17):
        b = blk[s_]
        fw.op("pool", lambda e: e.memset(vb1[:, s_, b:16], -BIG), writes=[vbb])
        if b + 1 < 16:
            fw.op("pool", lambda e: e.memset(vb2[:, s_, b + 1:16], -BIG), writes=[vbb])
        fw.op("pool", lambda e: e.memset(nown[:, s_, b:b + 1], 0.0), writes=[vbb])
    cvbc = cv[:, :].unsqueeze(1).to_broadcast([P, 17, 16])
    fw.op("pool", lambda e: e.tensor_tensor(vb1[:], vb1[:], cvbc, ALU.add), reads=[vbb, cvb], writes=[vbb])
    fw.op("pool", lambda e: e.tensor_tensor(vb2[:], vb2[:], cvbc, ALU.add), reads=[vbb, cvb], writes=[vbb])
    qx = T("m_qx", [P, NQ]); qxb = Buf("qx")
    kx = T("m_kx", [P, L]); kxb = Buf("kx")
    qTb = T("m_qTb", [P, NQ], BF16); qTbb = Buf("qTb")
    kTb = T("m_kTb", [P, L], BF16); kTbb = Buf("kTb")
    Vh = T("m_Vh", [P, 32, P], BF16); Vhb = Buf("Vh")
    km = T("m_km", [P, 16]); kmb_ = T("m_kmb", [P, 16], BF16); kmB = Buf("km")
    gt = T("m_gt", [P, 17, 16]); gtb = Buf("gt")
    mx = T("m_mx", [P, 17, 8]); mxb = Buf("mx")
    sel = T("m_sel", [P, 17, 16]); selb = Buf("sel")
    biasT = T("m_biasT", [16, NQ], BF16); biasTb = Buf("biasT")
    rt1 = Ring(fw, "m_rt1", 2, [32, 512], F32, st)
    pT = Ring(fw, "m_pT", 6, [P, 512], BF16, st)
    rl = T("m_rl", [P, 512]); rlb = Buf("rl")
    obt = Ring(fw, "m_ob", 2, [P, 512], BF16, st)
    psS = PsRing(fw, st, 6, (P, 512), F32, "mS")
    psO = fw.ps("mO", [P, 512], F32, st); psOb = Buf("mO")
    psL = fw.ps("mL", [P, 512], F32, st); psLb = Buf("mL")
    pm = K.cs("pm")[0:32, :]

    for h in range(8):
        fw.dma("sp", qx[:], K.qkB[h * P:(h + 1) * P, Q0:L], writes=[qxb], key="mq")
        fw.dma("sp", kx[:], K.qkB[1024 + h * P:1024 + (h + 1) * P, :], writes=[kxb], key="mk")
        fw.dma("sp", Vh[:], K.vB[:, h * P:(h + 1) * P].rearrange("(s p) d -> p s d", p=P), writes=[Vhb], key="mv")
        fw.flush()
        for (xt, xb_, n, tok0) in ((qx, qxb, NQ, Q0), (kx, kxb, L, 0)):
            for t0 in range(0, n, 512):
                tn = min(512, n - t0)
                ps, psb = psS.next()
                fw.op("pe", lambda e: e.matmul(ps[0:32, 0:tn], pm, xt[0:32, t0:t0 + tn], start=True, stop=True), reads=CS + [xb_], writes=[psb])
                r1, r1b, _ = rt1.next()
                fw.op("dve", lambda e: e.tensor_tensor(r1[:, 0:tn], ps[0:32, 0:tn], sinT[:, tok0 + t0:tok0 + t0 + tn], ALU.mult), reads=[psb, sinb], writes=[r1b])
                fw.op("dve", lambda e: e.tensor_tensor(xt[0:32, t0:t0 + tn], xt[0:32, t0:t0 + tn], cosT[:, tok0 + t0:tok0 + t0 + tn], ALU.mult),
                      reads=[xb_, cosb], writes=[xb_])
                fw.op("dve", lambda e: e.tensor_tensor(xt[0:32, t0:t0 + tn], xt[0:32, t0:t0 + tn], r1[:, 0:tn], ALU.add), reads=[xb_, r1b], writes=[xb_])
        fw.op("act", lambda e: e.activation(out=qTb[:], in_=qx[:], func=AF.Copy, scale=128.0 ** -0.5), reads=[qxb], writes=[qTbb])
        fw.op("act", lambda e: e.activation(out=kTb[:], in_=kx[:], func=AF.Copy), reads=[kxb], writes=[kTbb])
        fw.op("dve", lambda e: e.tensor_reduce(km[:], kx[:].rearrange("p (n b) -> p n b", b=256), AX.X, ALU.add), reads=[kxb], writes=[kmB])
        fw.op("dve", lambda e: e.tensor_scalar(kmb_[:], km[:], 1.0 / 256, None, ALU.mult), reads=[kmB], writes=[kmB])
        ps, psb = psS.next()
        for s_ in range(17):
            fw.op("pe", lambda e: e.matmul(ps[:, s_ * 16:(s_ + 1) * 16], qTb[:, s_ * P:(s_ + 1) * P], kmb_[:], start=True, stop=True),
                  reads=[qTbb, kmB], writes=[psb], inc=(s_ == 16))
        fw.op("dve", lambda e: e.tensor_tensor(gt[:], ps[:, 0:272].rearrange("p (s n) -> p s n", n=16), vb1[:], ALU.add), reads=[psb, vbb], writes=[gtb])
        for s_ in range(17):
            fw.op("dve", lambda e: e.max(mx[:, s_, :], gt[:, s_, :]), reads=[gtb], writes=[mxb])
        fw.op("dve", lambda e: e.tensor_tensor(sel[:], gt[:], mx[:, :, 2:3].to_broadcast([P, 17, 16]), ALU.is_ge), reads=[gtb, mxb], writes=[selb])
        fw.op("dve", lambda e: e.tensor_scalar(sel[:], sel[:], BIG, -BIG, ALU.mult, ALU.add), reads=[selb], writes=[selb])
        fw.op("dve", lambda e: e.tensor_tensor(sel[:], sel[:], vb2[:], ALU.add), reads=[selb, vbb], writes=[selb])
        fw.op("dve", lambda e: e.tensor_tensor(sel[:], sel[:], nown[:], ALU.mult), reads=[selb, vbb], writes=[selb])
        for g0 in range(0, 17, 4):
            g1 = min(17, g0 + 4)
            ps, psb = psS.next()
            for s_ in range(g0, g1):
                fw.op("pe", lambda e: e.transpose(ps[0:16, (s_ - g0) * P:(s_ - g0 + 1) * P], sel[:, s_, :], K.cs("ident")),
                      reads=[selb] + CS, writes=[psb], inc=(s_ == g1 - 1))
            fw.op("act", lambda e: e.activation(out=biasT[:, g0 * P:g1 * P], in_=ps[0:16, 0:(g1 - g0) * P], func=AF.Copy), reads=[psb], writes=[biasTb])
        for (t0, tn) in QTILES:
            kend = (Q0 + t0 + tn) // P
            pend = []

            def flush_one():
                ks_, pt_, ptb_ = pend.pop(0)
                fw.op("pe", lambda e: e.matmul(psO[:, 0:tn], Vh[:, ks_, :], pt_[:, 0:tn], start=(ks_ == 0), stop=(ks_ == kend - 1)),
                      reads=[Vhb, ptb_], writes=[psOb], inc=False)
                fw.op("pe", lambda e: e.matmul(psL[:, 0:tn], K.onesb[:], pt_[:, 0:tn], start=(ks_ == 0), stop=(ks_ == kend - 1)),
                      reads=[K.cbuf, ptb_], writes=[psLb], inc=True)

            for ks in range(kend):
                n = ks // 2
                diag = ks * P + P - 1 >= Q0 + t0
                ps, psb = psS.next()
                fw.op("pe", lambda e: e.matmul(ps[:, 0:tn], kTb[:, ks * P:(ks + 1) * P], qTb[:, t0:t0 + tn], start=True, stop=False),
                      reads=[kTbb, qTbb], writes=[psb], inc=False)
                fw.op("pe", lambda e: e.matmul(ps[:, 0:tn], K.Eb[0:16, n * P:(n + 1) * P], biasT[:, t0:t0 + tn], start=False, stop=not diag),
                      reads=[K.cbuf, biasTb], writes=[psb], inc=not diag)
                if diag:
                    o = (ks * P - (Q0 + t0)) // P
                    fw.op("pe", lambda e: e.matmul(ps[:, 0:tn], K.identb[:], K.cbb[:, o * 512:o * 512 + tn], start=False, stop=True),
                          reads=[K.cbuf], writes=[psb])
                pt_, ptb_, _ = pT.next()
                fw.op("act", lambda e: e.activation(out=pt_[:, 0:tn], in_=ps[:, 0:tn], func=AF.Exp), reads=[psb], writes=[ptb_])
                pend.append((ks, pt_, ptb_))
                if len(pend) > 4:
                    flush_one()
            while pend:
                flush_one()
            fw.op("dve", lambda e: e.reciprocal(rl[:, 0:tn], psL[:, 0:tn]), reads=[psLb], writes=[rlb])
            ot, otb, oi = obt.next()
            fw.op("dve", lambda e: e.tensor_tensor(ot[:, 0:tn], psO[:, 0:tn], rl[:, 0:tn], ALU.mult), reads=[psOb, rlb], writes=[otb])
            fw.dma("sp", K.obT[h * P:(h + 1) * P, t0:t0 + tn], ot[:, 0:tn], reads=[otb], key=f"mo{oi}", defer=True)


def phase_merge(K, fw, st):
    P = 128
    oa = fw.sb("mg_oa", [P, 8, NQ], BF16, st); oab = Buf("oa")
    ob = fw.sb("mg_ob", [P, 8, NQ], BF16, st); obb = Buf("ob")
    for kc in range(8):
        fw.dma("sp", oa[:, kc, :], K.oaT[kc * P:(kc + 1) * P, :], writes=[oab] if kc == 0 else [], key="mgoa")
        fw.dma("sp", ob[:, kc, :], K.obT[kc * P:(kc + 1) * P, :], writes=[obb] if kc == 0 else [], key="mgob")
    oab.w = ("mgoa", fw.dcnt["mgoa"]); obb.w = ("mgob", fw.dcnt["mgob"])
    wra = Ring(fw, "mg_wa", 2, [P, 8, 512], BF16, st)
    wrb = Ring(fw, "mg_wb", 2, [P, 8, 512], BF16, st)
    gr = Ring(fw, "mg_g", 4, [P, 512], F32, st)
    m1 = Ring(fw, "mg_m1", 2, [P, 512], F32, st)
    m2 = Ring(fw, "mg_m2", 2, [P, 512], F32, st)
    mo = Ring(fw, "mg_mo", 2, [P, 512], BF16, st)
    psr = PsRing(fw, st, 8)
    for c0 in range(0, D, 512):
        wa, wab = load_wslab(K, fw, wra, K.w_a, c0, 512, 8, "mgwa")
        wb, wbb = load_wslab(K, fw, wrb, K.w_b, c0, 512, 8, "mgwb")
        for j in range(4):
            r0 = c0 + j * P
            for (t0, tn) in QTILES:
                ga, gab, gi = gr.next()
                fw.dma("sp", ga[:, 0:tn], K.gA[r0:r0 + P, t0:t0 + tn], writes=[gab], key=f"mgg{gi}")
                gb, gbb, gi = gr.next()
                fw.dma("sp", gb[:, 0:tn], K.gB[r0:r0 + P, t0:t0 + tn], writes=[gbb], key=f"mgg{gi}")
                fw.flush()
                pa, pab = psr.next()
                for kc in range(8):
                    fw.op("pe", lambda e: e.matmul(pa[:, 0:tn], wa[:, kc, j * P:(j + 1) * P], oa[:, kc, t0:t0 + tn], start=(kc == 0), stop=(kc == 7)),
                          reads=[wab, oab], writes=[pab], inc=(kc == 7))
                pb, pbb = psr.next()
                for kc in range(8):
                    fw.op("pe", lambda e: e.matmul(pb[:, 0:tn], wb[:, kc, j * P:(j + 1) * P], ob[:, kc, t0:t0 + tn], start=(kc == 0), stop=(kc == 7)),
                          reads=[wbb, obb], writes=[pbb], inc=(kc == 7))
                a1, a1b, _ = m1.next(); a2, a2b, _ = m2.next(); o_, o_b, oi = mo.next()
                fw.op("dve", lambda e: e.tensor_tensor(a1[:, 0:tn], pa[:, 0:tn], ga[:, 0:tn], ALU.mult), reads=[pab, gab], writes=[a1b])
                fw.op("dve", lambda e: e.tensor_tensor(a2[:, 0:tn], pb[:, 0:tn], gb[:, 0:tn], ALU.mult), reads=[pbb, gbb], writes=[a2b])
                fw.op("dve", lambda e: e.tensor_tensor(o_[:, 0:tn], a1[:, 0:tn], a2[:, 0:tn], ALU.add), reads=[a1b, a2b], writes=[o_b])
                fw.dma("sp", K.mT[t0 // P:(t0 + tn) // P, :, r0 // P, :].rearrange("s p t -> p s t"),
                       o_[:, 0:tn].rearrange("p (s t) -> p s t", t=P), reads=[o_b], key=f"mgo{oi}", defer=True)


def phase_outproj(K, fw, st):
    P = 128
    wr = Ring(fw, "op_w", 2, [P, 16, 512], BF16, st)
    ms = Ring(fw, "op_m", 3, [P, 16, P], BF16, st)
    xr = Ring(fw, "op_x", 3, [P, 512], F32, st)
    orr = Ring(fw, "op_o", 3, [P, 512], F32, st)
    psr = PsRing(fw, st, 6)
    for g in range(4):
        wt, wb = load_wslab(K, fw, wr, K.w_out, g * 512, 512, 16, "opw")
        for s_ in range(17):
            mt, mb, mi = ms.next()
            fw.dma("sp", mt[:], K.mT[s_, :, :, :], writes=[mb], key=f"opm{mi}")
            xt, xb, xi = xr.next()
            fw.dma("sp", xt[:], K.x[Q0 + s_ * P:Q0 + (s_ + 1) * P, g * 512:(g + 1) * 512], writes=[xb], key=f"opx{xi}")
            fw.flush()
            pt, pb = psr.next()
            for kc in range(16):
                fw.op("pe", lambda e: e.matmul(pt[:, :], mt[:, kc, :], wt[:, kc, :], start=(kc == 0), stop=(kc == 15)),
                      reads=[mb, wb], writes=[pb], inc=(kc == 15))
            ot, ob, oi = orr.next()
            fw.op("dve", lambda e: e.tensor_tensor(ot[:], pt[:, :], xt[:], ALU.add), reads=[pb, xb], writes=[ob])
            fw.dma("sp", K.x2[s_ * P:(s_ + 1) * P, g * 512:(g + 1) * 512], ot[:], reads=[ob], key=f"opo{oi}", defer=True)


def phase_ffn_up(K, fw, st):
    P = 128
    h2 = fw.sb("fu_h2", [P, 16, NQ], BF16, st)
    h2b = [Buf(f"h2{i}") for i in range(17)]
    with ExitStack() as pst:
        pp = [(fw.ps(f"fptr{i}", [P, 2048], BF16, pst), Buf(f"fptr{i}")) for i in range(2)]
        norm_to_hT(K, fw, pst, K.x2, [i * P for i in range(17)], K.ln2, h2, h2b, pp, "n2")
        fw.barrier()
    fcw, fcwb = load_rows_T(K, fw, st, K.ffn_conv, 3, 2 * DFF, "fcw")
    fcb, fcbb = load_rows_T(K, fw, st, K.ffn_bias, 1, 2 * DFF, "fcb")
    wr = Ring(fw, "fu_w", 4, [P, 16, 256], BF16, st)
    ug = Ring(fw, "fu_ug", 2, [P, 2 + NQ], F32, st)
    uv = Ring(fw, "fu_uv", 2, [P, 2 + NQ], F32, st)
    for r in (ug, uv):
        for t, b_ in zip(r.t, r.b):
            fw.op("pool", lambda e: e.memset(t[:, 0:2], 0.0), writes=[b_])
    cgr = Ring(fw, "fu_cg", 2, [P, NQ], F32, st)
    cvr = Ring(fw, "fu_cv", 2, [P, NQ], F32, st)
    uo = Ring(fw, "fu_uo", 2, [P, NQ], BF16, st)
    psr = PsRing(fw, st, 8)

    def act_bufs(t0, tn):
        return h2b[t0 // P:(t0 + tn + P - 1) // P]

    for j2 in range(0, 44, 2):
        wg, wgb = load_wslab(K, fw, wr, K.w_up, j2 * P, 256, 16, "fuw")
        wv, wvb = load_wslab(K, fw, wr, K.w_up, DFF + j2 * P, 256, 16, "fuw")
        for jj in range(2):
            j = j2 + jj
            ugt, ugb, _ = ug.next(); uvt, uvb, _ = uv.next()
            for (wt, wb, ut, ub) in ((wg, wgb, ugt, ugb), (wv, wvb, uvt, uvb)):
                for (t0, tn) in QTILES:
                    pt, pb = psr.next()
                    for kc in range(16):
                        fw.op("pe", lambda e: e.matmul(pt[:, 0:tn], wt[:, kc, jj * P:(jj + 1) * P], h2[:, kc, t0:t0 + tn], start=(kc == 0), stop=(kc == 15)),
                              reads=[wb] + act_bufs(t0, tn), writes=[pb], inc=(kc == 15))
                    fw.op("act", lambda e: e.activation(out=ut[:, 2 + t0:2 + t0 + tn], in_=pt[:, 0:tn], func=AF.Copy), reads=[pb], writes=[ub])
            cg, cgb, _ = cgr.next(); cv, cvb, _ = cvr.next()
            for (ut, ub, ct, cb_, ch) in ((ugt, ugb, cg, cgb, j), (uvt, uvb, cv, cvb, 44 + j)):
                fw.op("dve", lambda e: e.tensor_scalar(ct[:], ut[:, 0:NQ], fcw[:, ch, 0:1], fcb[:, ch, 0:1], ALU.mult, ALU.add),
                      reads=[ub, fcwb, fcbb], writes=[cb_])
                for jt in (1, 2):
                    fw.op("dve", lambda e: e.scalar_tensor_tensor(ct[:], ut[:, jt:NQ + jt], fcw[:, ch, jt:jt + 1], ct[:], ALU.mult, ALU.add),
                          reads=[ub, fcwb, cb_], writes=[cb_])
            fw.op("act", lambda e: e.activation(out=cg[:], in_=cg[:], func=AF.Silu), reads=[cgb], writes=[cgb])
            ot, otb, oi = uo.next()
            fw.op("dve", lambda e: e.tensor_tensor(ot[:], cg[:], cv[:], ALU.mult), reads=[cgb, cvb], writes=[otb])
            fw.dma("sp", K.uT[:, :, j, :].rearrange("s p t -> p s t"), ot[:, :].rearrange("p (s t) -> p s t", t=P), reads=[otb], key=f"fuo{oi}")


def phase_ffn_down(K, fw, st):
    P = 128
    NKC = DFF // P
    wr = Ring(fw, "fd_w", 2, [P, NKC, 512], BF16, st)
    us = Ring(fw, "fd_u", 2, [P, NKC, P], BF16, st)
    xr = Ring(fw, "fd_x", 3, [P, 512], F32, st)
    orr = Ring(fw, "fd_o", 3, [P, 512], F32, st)
    psr = PsRing(fw, st, 6)
    for g in range(4):
        wt, wb = load_wslab(K, fw, wr, K.w_down, g * 512, 512, NKC, "fdw")
        for s_ in range(16):
            ut, ub, ui = us.next()
            fw.dma("sp", ut[:], K.uT[1 + s_, :, :, :], writes=[ub], key=f"fdu{ui}")
            xt, xb, xi = xr.next()
            fw.dma("sp", xt[:], K.x2[P + s_ * P:P + (s_ + 1) * P, g * 512:(g + 1) * 512], writes=[xb], key=f"fdx{xi}")
            fw.flush()
            pt, pb = psr.next()
            for kc in range(NKC):
                fw.op("pe", lambda e: e.matmul(pt[:, :], ut[:, kc, :], wt[:, kc, :], start=(kc == 0), stop=(kc == NKC - 1)),
                      reads=[ub, wb], writes=[pb], inc=(kc == NKC - 1))
            ot, ob, oi = orr.next()
            fw.op("dve", lambda e: e.tensor_tensor(ot[:], pt[:, :], xt[:], ALU.add), reads=[pb, xb], writes=[ob])
            fw.dma("sp", K.x3[s_ * P:(s_ + 1) * P, g * 512:(g + 1) * 512], ot[:], reads=[ob], key=f"fdo{oi}", defer=True)


def phase_final(K, fw, st):
    P = 128
    fn = fw.sb("fn_w", [P, D], F32, st); fnb = Buf("fnw")
    fw.dma("sp", fn[:], K.fnorm[0:1, :].partition_broadcast(P), writes=[fnb], key="fnw")
    xr = Ring(fw, "fn_x", 2, [P, D], F32, st)
    yr = Ring(fw, "fn_y", 2, [P, D], F32, st)
    junk = fw.sb("fn_junk", [P, D], BF16, st); junkb = Buf("junk")
    ss = Ring(fw, "fn_ss", 2, [P, 4], F32, st)
    for s_ in range(16):
        xt, xb, xi = xr.next()
        fw.dma("sp", xt[:], K.x3[s_ * P:(s_ + 1) * P, :], writes=[xb], key=f"fnx{xi}")
        fw.flush()
        sst, ssb, _ = ss.next()
        fw.op("act", lambda e: e.activation(out=junk[:], in_=xt[:], func=AF.Square, accum_out=sst[:, 0:1]), reads=[xb], writes=[junkb, ssb])
        fw.op("act", lambda e: e.activation(out=sst[:, 1:2], in_=sst[:, 0:1], func=AF.Ln, scale=1.0 / D, bias=K.epsc[:, 0:1]),
              reads=[ssb, K.epsb], writes=[ssb])
        fw.op("act", lambda e: e.activation(out=sst[:, 2:3], in_=sst[:, 1:2], func=AF.Exp, scale=-0.5), reads=[ssb], writes=[ssb])
        yt, yb, yi = yr.next()
        fw.op("dve", lambda e: e.scalar_tensor_tensor(yt[:], xt[:], sst[:, 2:3], fn[:], ALU.mult, ALU.mult), reads=[xb, ssb, fnb], writes=[yb])
        fw.dma("sp", K.out[s_ * P:(s_ + 1) * P, :], yt[:], reads=[yb], key=f"fno{yi}", defer=True)


def make_in_maps(inputs, cores=range(8)):
    x = np.asarray(inputs["x"], np.float32)
    pos = np.asarray(inputs["positions"], np.int32)
    shared = {
        "consts": CONST_ARR,
        "ln1": np.asarray(inputs["ln1"], np.float32).reshape(1, D),
        "ln2": np.asarray(inputs["ln2"], np.float32).reshape(1, D),
        "final_norm": np.asarray(inputs["final_norm"], np.float32).reshape(1, D),
        "w_in": np.ascontiguousarray(np.asarray(inputs["w_in"], np.float32)[0]),
        "gdn_conv": np.ascontiguousarray(np.asarray(inputs["gdn_conv"], np.float32)[0]),
        "gdn_a_log": np.asarray(inputs["gdn_a_log"], np.float32).reshape(1, 8),
        "gdn_dt_bias": np.asarray(inputs["gdn_dt_bias"], np.float32).reshape(1, 8),
        "gdn_norm": np.asarray(inputs["gdn_norm"], np.float32).reshape(1, 128),
        "w_branch_a": np.ascontiguousarray(np.asarray(inputs["w_branch_a"], np.float32)[0]),
        "w_branch_b": np.ascontiguousarray(np.asarray(inputs["w_branch_b"], np.float32)[0]),
        "w_out": np.ascontiguousarray(np.asarray(inputs["w_out"], np.float32)[0]),
        "w_up": np.ascontiguousarray(np.asarray(inputs["w_up"], np.float32)[0]),
        "ffn_conv": np.ascontiguousarray(np.asarray(inputs["ffn_conv"], np.float32)[0]),
        "ffn_conv_bias": np.asarray(inputs["ffn_conv_bias"], np.float32).reshape(1, 2 * DFF),
        "w_down": np.ascontiguousarray(np.asarray(inputs["w_down"], np.float32)[0]),
    }
    maps = []
    for c in cores:
        b, half = c // 2, c % 2
        m = dict(shared)
        if half == 1:
            m["x"] = np.ascontiguousarray(x[b])
            m["pos"] = np.ascontiguousarray(pos[b]).reshape(1, L)
            m["cvalid"] = np.zeros((1, 16), np.float32)
        else:
            xl = np.zeros((L, D), np.float32)
            xl[2048:] = x[b, :2048]
            pl = np.zeros((1, L), np.int32)
            pl[0, 2048:] = pos[b, :2048]
            m["x"] = xl
            m["pos"] = pl
            cv = np.zeros((1, 16), np.float32)
            cv[0, :8] = -BIG
            m["cvalid"] = cv
        maps.append(m)
    return maps


_NC_CACHE = {}


def kernel(**inputs):
    if "nc" not in _NC_CACHE:
        _NC_CACHE["nc"] = build_program()
    nc = _NC_CACHE["nc"]
    maps = make_in_maps(inputs)
    res = run_bass_kernel_spmd(nc, maps, core_ids=list(range(8)))
    out = np.zeros((4, 4096, D), np.float32)
    for c in range(8):
        b, half = c // 2, c % 2
        out[b, half * 2048:(half + 1) * 2048] = res.results[c]["out"]
    return out
```
